# Optimizing a Trainium2 kernel written in Bass

```python
import jax, jax.numpy as jnp
from jax import lax
import numpy as np

D_MODEL = 1024
BATCH = 8
SEQ = 2048
DEPTH = 1
DEC_BATCH = 128
DEC_SEQ = 1
PAST_LEN = 16384
PAGE_SIZE = 128

D_MIX = D_MODEL
D_GMLP = D_MIX // 2
GMLP_HEADS = 4
GMLP_HD = D_GMLP // GMLP_HEADS
CHUNK = 128
D_SSD = D_MIX - D_GMLP
SSD_HEAD_DIM = 64
SSD_HEADS = D_SSD // SSD_HEAD_DIM
SSD_GROUPS = 2
HEADS_PER_GROUP = SSD_HEADS // SSD_GROUPS
D_STATE = 128
CONV_W = 4
CONV_DIM = D_SSD + 2 * SSD_GROUPS * D_STATE
SSD_CHUNK = 128
N_MEM = 256
X_HEADS = 4
X_HD = D_MODEL // X_HEADS
D_FF = -(-8 * D_MODEL // (3 * 256)) * 256
D_IN = 2 * D_GMLP + D_SSD + CONV_DIM + SSD_HEADS
SPLITS = [D_GMLP, 2 * D_GMLP, 2 * D_GMLP + D_SSD, 2 * D_GMLP + D_SSD + CONV_DIM]
EPS = 1e-6

kernel_name = "hymba_gmlp_ssd_memxattn_step"

f32 = jnp.float32


def rmsnorm(x, g):
    xf = x.astype(f32)
    y = xf * lax.rsqrt(jnp.mean(xf * xf, -1, keepdims=True) + EPS)
    return (y * g.astype(f32)).astype(x.dtype)


def layernorm(x, g, b):
    xf = x.astype(f32)
    mu = jnp.mean(xf, -1, keepdims=True)
    xc = xf - mu
    y = xc * lax.rsqrt(jnp.mean(xc * xc, -1, keepdims=True) + EPS)
    return (y * g.astype(f32) + b.astype(f32)).astype(x.dtype)


def gated_group_norm(y, z, g):
    yz = y.astype(f32) * jax.nn.silu(z.astype(f32))
    shp = yz.shape
    yz = yz.reshape(*shp[:-1], SSD_GROUPS, D_SSD // SSD_GROUPS)
    yz = yz * lax.rsqrt(jnp.mean(yz * yz, -1, keepdims=True) + EPS)
    return yz.reshape(shp) * g.astype(f32)


def gmlp_mix(u, v, ws, bs):
    b, L, _ = v.shape
    T = min(L, CHUNK)
    nc = L // T
    mask = jnp.tril(jnp.ones((T, T), dtype=bool))
    w = jnp.where(mask, ws[:, :T, :T], jnp.zeros((), ws.dtype))
    vc = v.reshape(b, nc, T, GMLP_HEADS, GMLP_HD)
    mixed = jnp.einsum('hts,bcshd->bcthd', w, vc) + jnp.swapaxes(bs[:, :T], 0, 1)[:, :, None]
    return u * mixed.reshape(b, L, D_GMLP)


def causal_conv(xpad, w, bias, L):
    out = bias
    for k in range(CONV_W):
        out = out + xpad[:, k:k + L] * w[k]
    return jax.nn.silu(out)


def ssd_chunked(x, dt, A, Bm, Cm):
    b, L = x.shape[:2]
    Q = SSD_CHUNK
    nc = L // Q
    G, J, P, N = SSD_GROUPS, HEADS_PER_GROUP, SSD_HEAD_DIM, D_STATE
    xg = x.reshape(b, nc, Q, G, J, P)
    dtg = dt.reshape(b, nc, Q, G, J)
    Bc = Bm.reshape(b, nc, Q, G, N)
    Cc = Cm.reshape(b, nc, Q, G, N)
    a_cum = jnp.cumsum(dtg * A.reshape(G, J), axis=2)
    seg = a_cum[:, :, :, None] - a_cum[:, :, None, :]
    causal = jnp.tril(jnp.ones((Q, Q), dtype=bool))[:, :, None, None]
    decay = jnp.exp(jnp.where(causal, seg, -jnp.inf))
    CB = jnp.einsum('bctgn,bcsgn->bctsg', Cc, Bc)
    y_diag = jnp.einsum('bctsg,bctsgj,bcsgj,bcsgjp->bctgjp', CB, decay, dtg, xg)
    decay_end = jnp.exp(a_cum[:, :, -1:] - a_cum)
    states = jnp.einsum('bcsgn,bcsgj,bcsgjp->bcgjpn', Bc, decay_end * dtg, xg)
    chunk_decay = jnp.exp(a_cum[:, :, -1])

    def step(h, inp):
        dec, st = inp
        return h * dec[..., None, None] + st, h

    h0 = jnp.zeros((b, G, J, P, N), f32)
    h_final, h_prev = lax.scan(step, h0, (jnp.swapaxes(chunk_decay, 0, 1), jnp.swapaxes(states, 0, 1)))
    h_prev = jnp.swapaxes(h_prev, 0, 1)
    y_off = jnp.einsum('bctgn,bctgj,bcgjpn->bctgjp', Cc, jnp.exp(a_cum), h_prev)
    y = (y_diag + y_off).reshape(b, L, SSD_HEADS, P)
    return y, h_final.reshape(b, SSD_HEADS, P, N)


def ssd_recurrent(x, dt, A, Bm, Cm, h0):
    b, L = x.shape[:2]
    G, J, P, N = SSD_GROUPS, HEADS_PER_GROUP, SSD_HEAD_DIM, D_STATE
    Ag = A.reshape(G, J)
    xs = jnp.moveaxis(x.reshape(b, L, G, J, P), 1, 0)
    dts = jnp.moveaxis(dt.reshape(b, L, G, J), 1, 0)
    Bs = jnp.moveaxis(Bm, 1, 0)
    Cs = jnp.moveaxis(Cm, 1, 0)

    def step(h, inp):
        x_t, dt_t, B_t, C_t = inp
        h = h * jnp.exp(dt_t * Ag)[..., None, None] + jnp.einsum('bgj,bgjp,bgn->bgjpn', dt_t, x_t, B_t)
        y = jnp.einsum('bgn,bgjpn->bgjp', C_t, h)
        return h, y

    h, ys = lax.scan(step, h0.reshape(b, G, J, P, N), (xs, dts, Bs, Cs))
    y = jnp.moveaxis(ys, 0, 1).reshape(b, L, SSD_HEADS, P)
    return y, h.reshape(b, SSD_HEADS, P, N)


def mixer_block(hn, conv_prefix, h0, w_in, ln_g, ln_b, ws, bs, cw, cb, dtb, alog, dsk, sn, wo):
    b, L, _ = hn.shape
    proj = hn @ w_in
    u, v, z, xbc, dt_raw = jnp.split(proj, SPLITS, axis=-1)
    u = jax.nn.gelu(u)
    v = layernorm(jax.nn.gelu(v), ln_g, ln_b)
    g_out = gmlp_mix(u, v, ws, bs)
    xpad = jnp.concatenate([conv_prefix.astype(xbc.dtype), xbc], axis=1)
    xbc_c = causal_conv(xpad, cw, cb, L)
    xs, Bm, Cm = jnp.split(xbc_c, [D_SSD, D_SSD + SSD_GROUPS * D_STATE], axis=-1)
    xs = xs.reshape(b, L, SSD_HEADS, SSD_HEAD_DIM).astype(f32)
    Bm = Bm.reshape(b, L, SSD_GROUPS, D_STATE).astype(f32)
    Cm = Cm.reshape(b, L, SSD_GROUPS, D_STATE).astype(f32)
    dt = jax.nn.softplus(dt_raw.astype(f32) + dtb.astype(f32))
    A = -jnp.exp(alog.astype(f32))
    if h0 is None:
        y, h = ssd_chunked(xs, dt, A, Bm, Cm)
    else:
        y, h = ssd_recurrent(xs, dt, A, Bm, Cm, h0.astype(f32))
    y = y + dsk.astype(f32)[:, None] * xs
    s_out = gated_group_norm(y.reshape(b, L, D_SSD), z, sn).astype(hn.dtype)
    out = jnp.concatenate([g_out, s_out], axis=-1) @ wo
    return out, xpad[:, -(CONV_W - 1):], h.astype(hn.dtype), v


def mem_kv(mem, g, w_k, w_v):
    mn = rmsnorm(mem, g)
    b, M, _ = mem.shape
    return (mn @ w_k).reshape(b, M, X_HEADS, X_HD), (mn @ w_v).reshape(b, M, X_HEADS, X_HD)


def cross_attn(hn, k, v, w_q, w_xo):
    b, L, _ = hn.shape
    q = (hn @ w_q).reshape(b, L, X_HEADS, X_HD)
    s = jnp.einsum('bthd,bmhd->bhtm', q.astype(f32), k.astype(f32)) * (X_HD ** -0.5)
    p = jax.nn.softmax(s, axis=-1)
    o = jnp.einsum('bhtm,bmhd->bthd', p, v.astype(f32)).astype(hn.dtype)
    return o.reshape(b, L, D_MODEL) @ w_xo


def swiglu(hn, w_gate, w_up, w_down):
    return (jax.nn.silu(hn @ w_gate) * (hn @ w_up)) @ w_down


def setup_inputs(seed: int = 0) -> dict:
    key = jax.random.key(seed)
    ks = iter(jax.random.split(key, 48))

    def nrm(shape, scale):
        return jax.random.normal(next(ks), shape, f32) * scale

    def gain(n):
        return 1.0 + nrm((DEPTH, n), 0.02)

    dt0 = jnp.exp(jax.random.uniform(next(ks), (DEPTH, SSD_HEADS), f32, np.log(1e-3), np.log(1e-1)))
    dt_bias = dt0 + jnp.log(-jnp.expm1(-dt0))
    a_log = jnp.log(jax.random.uniform(next(ks), (DEPTH, SSD_HEADS), f32, 1.0, 16.0))
    return {
        "x_prompt": nrm((BATCH, SEQ, D_MODEL), 1.0),
        "x_sample": nrm((DEC_BATCH, DEC_SEQ, D_MODEL), 1.0),
        "state_ssm": nrm((DEPTH, DEC_BATCH, SSD_HEADS, SSD_HEAD_DIM, D_STATE), 0.1),
        "state_conv": nrm((DEPTH, DEC_BATCH, CONV_W - 1, CONV_DIM), 1.0),
        "cache_mem_k": nrm((DEPTH, DEC_BATCH, N_MEM, X_HEADS, X_HD), 1.0),
        "cache_mem_v": nrm((DEPTH, DEC_BATCH, N_MEM, X_HEADS, X_HD), 1.0),
        "mem_prompt": nrm((BATCH, N_MEM, D_MODEL), 1.0),
        "norm_mix": gain(D_MODEL),
        "w_in": nrm((DEPTH, D_MODEL, D_IN), D_MODEL ** -0.5),
        "gmlp_ln_g": gain(D_GMLP),
        "gmlp_ln_b": nrm((DEPTH, D_GMLP), 0.02),
        "gmlp_ws": nrm((DEPTH, GMLP_HEADS, CHUNK, CHUNK), CHUNK ** -0.5),
        "gmlp_bs": 1.0 + nrm((DEPTH, GMLP_HEADS, CHUNK), 0.02),
        "conv_w": nrm((DEPTH, CONV_W, CONV_DIM), CONV_W ** -0.5),
        "conv_b": nrm((DEPTH, CONV_DIM), 0.02),
        "dt_bias": dt_bias,
        "a_log": a_log,
        "d_skip": 1.0 + nrm((DEPTH, SSD_HEADS), 0.02),
        "ssd_norm": gain(D_SSD),
        "w_out": nrm((DEPTH, D_MIX, D_MODEL), D_MIX ** -0.5),
        "norm_xattn": gain(D_MODEL),
        "norm_mem": gain(D_MODEL),
        "w_q": nrm((DEPTH, D_MODEL, D_MODEL), D_MODEL ** -0.5),
        "w_k": nrm((DEPTH, D_MODEL, D_MODEL), D_MODEL ** -0.5),
        "w_v": nrm((DEPTH, D_MODEL, D_MODEL), D_MODEL ** -0.5),
        "w_xo": nrm((DEPTH, D_MODEL, D_MODEL), D_MODEL ** -0.5),
        "norm_ffn": gain(D_MODEL),
        "w_gate": nrm((DEPTH, D_MODEL, D_FF), D_MODEL ** -0.5),
        "w_up": nrm((DEPTH, D_MODEL, D_FF), D_MODEL ** -0.5),
        "w_down": nrm((DEPTH, D_FF, D_MODEL), D_FF ** -0.5),
        "norm_final": 1.0 + nrm((D_MODEL,), 0.02),
    }


def reference(x_prompt, x_sample, state_ssm, state_conv, cache_mem_k, cache_mem_v, mem_prompt,
              norm_mix, w_in, gmlp_ln_g, gmlp_ln_b, gmlp_ws, gmlp_bs, conv_w, conv_b, dt_bias,
              a_log, d_skip, ssd_norm, w_out, norm_xattn, norm_mem, w_q, w_k, w_v, w_xo,
              norm_ffn, w_gate, w_up, w_down, norm_final):
    hp, hs = x_prompt, x_sample
    ssm_p, conv_p, gv_p, mk_p, mv_p = [], [], [], [], []
    ssm_s, conv_s, gv_s = [], [], []
    for l in range(DEPTH):
        mix_w = (w_in[l], gmlp_ln_g[l], gmlp_ln_b[l], gmlp_ws[l], gmlp_bs[l], conv_w[l], conv_b[l],
                 dt_bias[l], a_log[l], d_skip[l], ssd_norm[l], w_out[l])
        zero_prefix = jnp.zeros((hp.shape[0], CONV_W - 1, CONV_DIM), hp.dtype)
        out, cst, hst, v = mixer_block(rmsnorm(hp, norm_mix[l]), zero_prefix, None, *mix_w)
        hp = hp + out
        mk, mv = mem_kv(mem_prompt, norm_mem[l], w_k[l], w_v[l])
        hp = hp + cross_attn(rmsnorm(hp, norm_xattn[l]), mk, mv, w_q[l], w_xo[l])
        hp = hp + swiglu(rmsnorm(hp, norm_ffn[l]), w_gate[l], w_up[l], w_down[l])
        ssm_p.append(hst); conv_p.append(cst); gv_p.append(v[:, -CHUNK:]); mk_p.append(mk); mv_p.append(mv)
        out, cst, hst, v = mixer_block(rmsnorm(hs, norm_mix[l]), state_conv[l], state_ssm[l], *mix_w)
        hs = hs + out
        hs = hs + cross_attn(rmsnorm(hs, norm_xattn[l]), cache_mem_k[l], cache_mem_v[l], w_q[l], w_xo[l])
        hs = hs + swiglu(rmsnorm(hs, norm_ffn[l]), w_gate[l], w_up[l], w_down[l])
        ssm_s.append(hst); conv_s.append(cst); gv_s.append(v)
    y_prompt = rmsnorm(hp, norm_final)
    y_sample = rmsnorm(hs, norm_final)
    return (y_prompt, y_sample, jnp.stack(ssm_p), jnp.stack(conv_p), jnp.stack(gv_p),
            jnp.stack(mk_p), jnp.stack(mv_p), jnp.stack(ssm_s), jnp.stack(conv_s), jnp.stack(gv_s))
```

```python
import numpy as np
from contextlib import ExitStack
import concourse.bass as bass
import concourse.mybir as mybir
from concourse.bass_utils import run_bass_kernel_spmd

F32 = mybir.dt.float32
BF16 = mybir.dt.bfloat16
AF = mybir.ActivationFunctionType
ALU = mybir.AluOpType
AX = mybir.AxisListType

NCORES = 8
P = 128
D = 1024
KC = 8
SEQ = 2048
DIN = 2568
DFF = 2816
NJ = DFF // 128
NS = 16
NMEM = 256
EPS = 1e-6
TB = 128
TBX = 512
GELU = AF.Gelu_apprx_tanh

DEBUG = False
STOP = None


class StopBuild(Exception):
    pass


def ckpt(name):
    if STOP == name:
        raise StopBuild()


class Tok:
    __slots__ = ("sem", "val")

    def __init__(self, sem, val):
        self.sem = sem
        self.val = val


class Res:
    __slots__ = ("w", "r", "dsem", "dcnt", "name", "excl")

    def __init__(self, name=""):
        self.excl = False
        self.w = None
        self.r = []
        self.dsem = None
        self.dcnt = 0
        self.name = name


class Buf:
    def __init__(self, t, name):
        self.t = t
        self.r = Res(name)


class Eng:
    def __init__(self, pg, name, h):
        self.pg = pg
        self.name = name
        self.h = h
        self.sem = pg.newsem("e_" + name)
        pg.sem2eng[id(self.sem)] = self
        self.cnt = 0
        self.waited = {}
        self.nins = 0

    def wait(self, tok):
        if tok is None:
            return
        k = id(tok.sem)
        if self.waited.get(k, 0) >= tok.val:
            return
        prod = self.pg.sem2eng.get(k)
        if prod is not None and tok.val > prod.cnt:
            raise RuntimeError(f"wait on future token of {prod.name} from {self.name}: potential deadlock")
        self.h.wait_ge(tok.sem, tok.val)
        self.waited[k] = tok.val


class PG:
    def __init__(self, nc, es):
        self.nc = nc
        self.es = es
        self.nsem = 0
        self.sem2eng = {}
        self.PE = Eng(self, "pe", nc.tensor)
        self.ACT = Eng(self, "act", nc.scalar)
        self.DVE = Eng(self, "dve", nc.vector)
        self.POOL = Eng(self, "pool", nc.gpsimd)
        self.SP = Eng(self, "sp", nc.sync)
        self.dma_res = []
        self.nbuf = 0
        self.scopes = [es]

    def push(self):
        e = ExitStack()
        self.scopes.append(e)

    def pop(self):
        self.barrier()
        self.scopes.pop().close()

    def barrier(self):
        engs = [self.PE, self.ACT, self.DVE, self.POOL, self.SP]
        toks = [Tok(X.sem, X.cnt) for X in engs if X.cnt > 0]
        toks += [Tok(R.dsem, R.dcnt) for R in self.dma_res]
        for E in engs:
            for t in toks:
                E.wait(t)

    def newsem(self, name):
        self.nsem += 1
        return self.es.enter_context(self.nc.semaphore(name))

    def sb(self, name, shape, dt):
        t = self.scopes[-1].enter_context(self.nc.sbuf_tensor("s_" + name, list(shape), dt))
        if DEBUG:
            nb = int(np.prod(shape[1:])) * (2 if dt == BF16 else 4)
            self.nbuf += nb
            print("SB", name, shape, nb, "cum", self.nbuf, "depth", len(self.scopes))
        return Buf(t, name)

    def ps(self, name, shape, dt):
        t = self.es.enter_context(self.nc.psum_tensor("p_" + name, list(shape), dt))
        return Buf(t, name)

    @staticmethod
    def _addread(res, tok):
        for i, t in enumerate(res.r):
            if t.sem is tok.sem:
                res.r[i] = tok
                return
        res.r.append(tok)

    def op(self, E, emit, reads=(), writes=(), signal=True):
        ex = [r for r in reads if r.excl]
        if ex:
            reads = [r for r in reads if not r.excl]
            writes = list(writes) + [r for r in ex if r not in writes]
        for r in reads:
            if r.w is not None:
                E.wait(r.w)
        for w in writes:
            if w.w is not None and w.w.sem is not E.sem:
                E.wait(w.w)
            for t in w.r:
                if t.sem is not E.sem:
                    E.wait(t)
        ins = emit()
        E.nins += 1
        if signal:
            E.cnt += 1
            ins.then_inc(E.sem, 1)
            tok = Tok(E.sem, E.cnt)
        else:
            tok = Tok(E.sem, E.cnt + 1)
        for w in writes:
            w.w = tok
            w.r = []
        for r in reads:
            self._addread(r, tok)
        return tok

    def dma(self, Q, out, in_, reads=(), writes=(), semres=None):
        for r in reads:
            if r.w is not None:
                Q.wait(r.w)
        for w in writes:
            if w.w is not None:
                Q.wait(w.w)
            for t in w.r:
                Q.wait(t)
        R = semres if semres is not None else (writes[0] if writes else reads[0])
        if R.dsem is None:
            R.dsem = self.newsem("d_" + R.name)
            self.dma_res.append(R)
        R.dcnt += 16
        Q.h.dma_start(out=out, in_=in_).then_inc(R.dsem, 16)
        Q.nins += 1
        tok = Tok(R.dsem, R.dcnt)
        for w in writes:
            w.w = tok
            w.r = []
        for r in reads:
            self._addread(r, tok)
        return tok

    def finish(self, E):
        for R in self.dma_res:
            E.wait(Tok(R.dsem, R.dcnt))
        for X in (self.PE, self.ACT, self.DVE, self.POOL):
            if X is not E and X.cnt > 0:
                E.wait(Tok(X.sem, X.cnt))


class Ring:
    def __init__(self, bufs):
        self.bufs = bufs
        self.i = 0

    def next(self):
        b = self.bufs[self.i % len(self.bufs)]
        self.i += 1
        return b


C_ID, C_U, C_NEG, C_ONE = 0, 128, 256, 384
CW = 512


def _consts():
    c = np.zeros((P, CW), np.float32)
    c[:, C_ID:C_ID + 128] = np.eye(128, dtype=np.float32)
    s = np.arange(128)[:, None]
    t = np.arange(128)[None, :]
    c[:, C_U:C_U + 128] = (s <= t).astype(np.float32)
    c[:, C_NEG:C_NEG + 128] = np.where(s <= t, 0.0, -1e30).astype(np.float32)
    c[:, C_ONE:C_ONE + 128] = 1.0
    return c


R_LNG, R_LNB, R_DTB, R_ALOG, R_BS = 0, 512, 1024, 1032, 1040
RW = 1040 + 512
PC_GMIX, PC_GX, PC_GF, PC_GM, PC_CW, PC_CB, PC_SN, PC_D, PC_W00, PC_B0 = 0, 8, 16, 24, 32, 64, 72, 76, 80, 84
PCW = 88


def _tables(inp):
    row = np.concatenate([inp["gmlp_ln_g"][0], inp["gmlp_ln_b"][0], inp["dt_bias"][0], inp["a_log"][0],
                          inp["gmlp_bs"][0].reshape(-1)]).astype(np.float32)
    prow = np.ascontiguousarray(np.broadcast_to(row[None, :], (P, RW)))
    pc = np.zeros((P, PCW), np.float32)

    def col(v):
        return np.asarray(v, np.float32).reshape(-1, P).T

    pc[:, PC_GMIX:PC_GMIX + 8] = col(inp["norm_mix"][0])
    pc[:, PC_GX:PC_GX + 8] = col(inp["norm_xattn"][0])
    pc[:, PC_GF:PC_GF + 8] = col(inp["norm_ffn"][0])
    pc[:, PC_GM:PC_GM + 8] = col(inp["norm_mem"][0])
    for k in range(4):
        pc[:, PC_CW + 8 * k:PC_CW + 8 * k + 8] = col(inp["conv_w"][0, k])
    pc[:, PC_CB:PC_CB + 8] = col(inp["conv_b"][0])
    pc[:, PC_SN:PC_SN + 4] = col(inp["ssd_norm"][0])
    pc[:, PC_D:PC_D + 4] = col(np.repeat(inp["d_skip"][0], 64))
    pc[:, PC_W00:PC_W00 + 4] = col(np.repeat(inp["gmlp_ws"][0, :, 0, 0], 128))
    pc[:, PC_B0:PC_B0 + 4] = col(np.repeat(inp["gmlp_bs"][0, :, 0], 128))
    nf = np.ascontiguousarray(np.broadcast_to(inp["norm_final"][None, :].astype(np.float32), (P, D)))
    return prow, pc, nf


def build_program():
    st_ = {}
    try:
        return _build(st_)
    except StopBuild:
        return st_["finalize"]()


def _build(st_):
    nc = bass.Bass("TRN2", target_bir_lowering=False)
    es = ExitStack()
    pg = PG(nc, es)
    PE, ACT, DVE, POOL, SP = pg.PE, pg.ACT, pg.DVE, pg.POOL, pg.SP

    def din(name, shape, dt=F32):
        return nc.dram_tensor(name, list(shape), dt, kind="ExternalInput").ap()

    def dout(name, shape, dt=F32):
        return nc.dram_tensor(name, list(shape), dt, kind="ExternalOutput").ap()

    xp = din("xp", [SEQ, D])
    memp = din("memp", [NMEM, D])
    xs = din("xs", [NS, D])
    sssm = din("sssm", [NS, 512, 128])
    sconv = din("sconv", [NS * 3, D])
    ck = din("ck", [NS, NMEM, D])
    cv = din("cv", [NS, NMEM, D])
    w_in = din("w_in", [D, DIN])
    w_out = din("w_out", [D, D])
    w_q = din("w_q", [D, D])
    w_k = din("w_k", [D, D])
    w_v = din("w_v", [D, D])
    w_xo = din("w_xo", [D, D])
    w_gate = din("w_gate", [D, DFF])
    w_up = din("w_up", [D, DFF])
    w_down = din("w_down", [DFF, D])
    gws = din("gws", [4 * 128, 128])
    consts_d = din("consts", [P, CW])
    prow_d = din("prow", [P, RW])
    pcol_d = din("pcol", [P, PCW])
    nf_d = din("nfrow", [P, D])

    y_prompt = dout("y_prompt", [SEQ, D])
    y_sample = dout("y_sample", [NS, D])
    ssm_prompt = dout("ssm_prompt", [512, 128])
    conv_prompt = dout("conv_prompt", [3, D])
    gv_prompt = dout("gv_prompt", [128, 512])
    mk_o = dout("mk_o", [NMEM, D])
    mv_o = dout("mv_o", [NMEM, D])
    ssm_sample = dout("ssm_sample", [NS, 512, 128])
    conv_sample = dout("conv_sample", [NS * 3, D])
    gv_sample = dout("gv_sample", [NS, 512])
    dbg_o = dout("dbg", [P, KC * SEQ]) if DEBUG else None
    dbgs_o = dout("dbgs", [P, KC * NS]) if DEBUG else None
    dbgs2_o = dout("dbg2", [P, 16384]) if DEBUG else None

    def rl(xs_):
        out = []
        for x in xs_:
            out.append(x.r if isinstance(x, Buf) else x)
        return out

    def mm(out, lhsT, rhs, start, stop, R, W, sig=True):
        return pg.op(PE, lambda: nc.tensor.matmul(out, lhsT=lhsT, rhs=rhs, start=start, stop=stop),
                     rl(R), rl(W), sig)

    def tr(out, in_, ident, R, W, sig=True):
        return pg.op(PE, lambda: nc.tensor.transpose(out, in_, ident), rl(R), rl(W), sig)

    def act(out, in_, func, R, W, bias=None, scale=None, accum=None):
        kw = {}
        if bias is not None:
            kw["bias"] = bias
        if scale is not None:
            kw["scale"] = scale
        if accum is not None:
            kw["accum_out"] = accum
        return pg.op(ACT, lambda: nc.scalar.activation(out=out, in_=in_, func=func, **kw), rl(R), rl(W))

    def eh(E):
        return nc.vector if E is DVE else nc.gpsimd

    def tt(E, out, in0, in1, op, R, W):
        return pg.op(E, lambda: eh(E).tensor_tensor(out=out, in0=in0, in1=in1, op=op), rl(R), rl(W))

    def ts(E, out, in0, s1, s2, op0, op1, R, W):
        if op1 is None:
            return pg.op(E, lambda: eh(E).tensor_scalar(out=out, in0=in0, scalar1=s1, scalar2=None, op0=op0),
                         rl(R), rl(W))
        return pg.op(E, lambda: eh(E).tensor_scalar(out=out, in0=in0, scalar1=s1, scalar2=s2, op0=op0, op1=op1),
                     rl(R), rl(W))

    def stt(out, in0, scalar, in1, op0, op1, R, W, accum=None):
        if accum is None:
            return pg.op(DVE, lambda: nc.vector.scalar_tensor_tensor(out=out, in0=in0, scalar=scalar, in1=in1,
                                                                     op0=op0, op1=op1), rl(R), rl(W))
        return pg.op(DVE, lambda: nc.vector.scalar_tensor_tensor(out=out, in0=in0, scalar=scalar, in1=in1,
                                                                 op0=op0, op1=op1, accum_out=accum), rl(R), rl(W))

    def cp(E, out, in_, R, W):
        if E is ACT:
            return pg.op(ACT, lambda: nc.scalar.copy(out=out, in_=in_), rl(R), rl(W))
        return pg.op(E, lambda: eh(E).tensor_copy(out=out, in_=in_), rl(R), rl(W))

    def recip(out, in_, R, W):
        return pg.op(DVE, lambda: nc.vector.reciprocal(out=out, in_=in_), rl(R), rl(W))

    def mset(E, ap, val, W):
        return pg.op(E, lambda: eh(E).memset(ap, val), [], rl(W))

    def ld(out, in_, W, R=(), semres=None):
        return pg.dma(SP, out, in_, rl(R), rl(W), semres)

    def ldc(out, in_, W, R=(), semres=None):
        return pg.dma(POOL, out, in_, rl(R), rl(W), semres)

    def st(out, in_, R, semres):
        return pg.dma(SP, out, in_, rl(R), [], semres)

    def bc(ap, shape):
        return ap.broadcast_to(list(shape))

    setup = Res("setup")
    outsem = Res("outs")
    cst = pg.sb("cst", [P, CW], F32)
    prow = pg.sb("prow", [P, RW], F32)
    pcol = pg.sb("pcol", [P, PCW], F32)
    ld(cst.t[:, :], consts_d[:, :], [cst])
    ld(prow.t[:, :], prow_d[:, :], [prow])
    ld(pcol.t[:, :], pcol_d[:, :], [pcol])
    identf = cst.t[:, C_ID:C_ID + 128]
    Umat = cst.t[:, C_U:C_U + 128]
    negm = cst.t[:, C_NEG:C_NEG + 128]
    onesf = cst.t[:, C_ONE:C_ONE + 128]
    cb16 = pg.sb("cb16", [P, 256], BF16)
    cp(DVE, cb16.t[:, 0:128], identf, [cst], [cb16])
    cp(DVE, cb16.t[:, 128:256], onesf, [cst], [cb16])
    identb = cb16.t[:, 0:128]
    onesb = cb16.t[:, 128:256]
    arow = pg.sb("arow", [P, 8], F32)
    act(arow.t[:, :], prow.t[:, R_ALOG:R_ALOG + 8], AF.Exp, [prow], [arow])
    ts(DVE, arow.t[:, :], arow.t[:, :], -1.0, None, ALU.mult, None, [arow], [arow])

    PSB = [pg.ps(f"psb{i}", [P, 512], F32) for i in range(8)]
    for b_ in PSB:
        b_.r.excl = True
    psA = Ring(PSB[0:4])
    psF = Ring(PSB[0:2])
    psK = Ring(PSB[2:4])
    psB = Ring(PSB[4:6])
    psC = Ring(PSB[6:8])

    hpT = pg.sb("hpT", [P, KC, SEQ], F32)
    hpR = [Res(f"hp{i}") for i in range(SEQ // TB)]

    def hpres(t0, n):
        return [hpR[i] for i in range(t0 // TB, (t0 + n + TB - 1) // TB)]

    hsT = pg.sb("hsT", [P, KC, NS], F32)
    hsR = [Res("hs")]

    def finalize():
        if DEBUG:
            pg.dma(SP, dbg_o[:, :], hpT.t[:, :, :].rearrange("p k t -> p (k t)"), hpR, [], outsem)
            pg.dma(SP, dbgs_o[:, :], hsT.t[:, :, :].rearrange("p k t -> p (k t)"), hsR, [], outsem)
            if STOP is not None and STOP.startswith("block") and "dumps" in st_:
                off = 0
                for nm, b_, n_ in st_["dumps"]():
                    ap_ = b_.t[:, :] if len(b_.t.shape) == 2 else (b_.t[:, :, :].rearrange("p a b -> p (a b)"))
                    pg.dma(POOL if b_.t.dtype == BF16 else SP, dbgs2_o[:, off:off + n_], ap_, [b_.r], [], outsem)
                    print("DUMP", nm, off, n_)
                    off += n_
        while len(pg.scopes) > 1:
            pg.pop()
        pg.finish(SP)
        es.close()
        return nc
    st_["finalize"] = finalize

    WT = pg.sb("WT", [P, 4, 128], BF16)
    pg.push()
    wsraw = pg.sb("wsraw", [P, 4, 128], F32)
    ld(wsraw.t[:, :, :], gws.rearrange("(h t) s -> t h s", t=128), [wsraw])
    pb = psC.next()
    for h in range(4):
        tr(pb.t[:, h * 128:(h + 1) * 128], wsraw.t[:, h, :], identf, [wsraw, cst], [pb], sig=(h == 3))
    tt(DVE, WT.t[:, :, :], pb.t[:, :].rearrange("p (h t) -> p h t", h=4),
       bc(Umat.unsqueeze(1), [P, 4, 128]), ALU.mult, [pb, cst], [WT])
    pg.pop()

    rings = {}

    def mk_rings(tag, n):
        rings["sq"] = Ring([pg.sb(f"sq{tag}{i}", [P, n], BF16) for i in range(2)])
        rings["std"] = Ring([pg.sb(f"std{tag}{i}", [P, n], F32) for i in range(2)])

    def rms_rstd(src, sres, t0, n, nfeat_tiles=KC, denom=D):
        pb_ = psC.next()
        if n <= 128 and "sq8" in rings:
            sq = rings["sq8"].next()
            act(sq.t[:, :, 0:n], src[:, :, t0:t0 + n], AF.Square, sres, [sq])
            for k in range(nfeat_tiles):
                mm(pb_.t[:, 0:n], onesb, sq.t[:, k, 0:n], k == 0, k == nfeat_tiles - 1, [sq, cb16], [pb_],
                   sig=(k == nfeat_tiles - 1))
        else:
            for k in range(nfeat_tiles):
                sq = rings["sq"].next()
                act(sq.t[:, 0:n], src[:, k, t0:t0 + n], AF.Square, sres, [sq])
                mm(pb_.t[:, 0:n], onesb, sq.t[:, 0:n], k == 0, k == nfeat_tiles - 1, [sq, cb16], [pb_], sig=True)
        ckpt("n_mm")
        sd = rings["std"].next()
        act(sd.t[:, 0:n], pb_.t[:, 0:n], AF.Ln, [pb_], [sd], bias=EPS, scale=1.0 / denom)
        act(sd.t[:, 0:n], sd.t[:, 0:n], AF.Exp, [sd], [sd], scale=-0.5)
        return sd

    def norm_to(dst, dres, src, sres, t0, n, gcol0, d0=0):
        rs = rms_rstd(src, sres, t0, n)
        for k in range(KC):
            stt(dst[:, k, d0:d0 + n], src[:, k, t0:t0 + n], pcol.t[:, gcol0 + k:gcol0 + k + 1], rs.t[:, 0:n],
                ALU.mult, ALU.mult, sres + [pcol, rs], dres)

    def proj_fm(W, Wres, c0, hn, hnres, n, consume, kin=KC, ring=None):
        pb_ = (ring or psA).next()
        for k in range(kin):
            mm(pb_.t[:, 0:n], W[:, k, c0:c0 + 128], hn[:, k, 0:n], k == 0, k == kin - 1, Wres + hnres, [pb_],
               sig=(k == kin - 1))
        consume(pb_)

    pg.push()
    mk_rings("m", 128)
    rings["sq8"] = Ring([pg.sb(f"sq8{i}", [P, KC, 128], BF16) for i in range(1)])
    win = pg.sb("win", [P, KC, DIN], BF16)
    wout = pg.sb("wout", [P, KC, D], BF16)
    w_in_v = w_in.rearrange("(k p) n -> p k n", p=P)
    winR = [Res(f"win{c0}") for c0 in range(0, DIN, 256)]
    for i_ in (10, 0, 1, 4, 5, 2, 3, 6, 7, 8, 9):
        c0 = i_ * 256
        c1 = min(DIN, c0 + 256)
        ldc(win.t[:, :, c0:c1], w_in_v[:, :, c0:c1], [winR[i_]])
    w_out_v = w_out.rearrange("(k p) n -> p k n", p=P)
    woutR = []
    for c0 in range(0, D, 256):
        r = Res(f"wout{c0}")
        woutR.append(r)
        ldc(wout.t[:, :, c0:c0 + 256], w_out_v[:, :, c0:c0 + 256], [r])

    def winres(c0, c1):
        return [winR[i] for i in range(c0 // 256, (c1 - 1) // 256 + 1)]

    xin_ring = Ring([pg.sb(f"xin{i}", [P, D], F32) for i in range(1)])

    def load_tokens_fm(src_dram, row0, nrows, dst, dres, dcol0, preloaded=False):
        xin = xin_ring.bufs[0]
        if not preloaded:
            ld(xin.t[0:nrows, :], src_dram[row0:row0 + nrows, :], [xin])
        for half in range(2):
            pb_ = psC.next()
            for q in range(4):
                k = half * 4 + q
                tr(pb_.t[:, q * 128:q * 128 + nrows], xin.t[0:nrows, k * 128:(k + 1) * 128],
                   identf[0:nrows, 0:nrows], [xin, cst], [pb_], sig=(q == 3))
            cp(ACT if half == 0 else DVE, dst[:, half * 4:half * 4 + 4, dcol0:dcol0 + nrows],
               pb_.t[:, :].rearrange("p (q t) -> p q t", q=4)[:, :, 0:nrows], [pb_], dres)

    sm = Ring([pg.sb(f"sm{i}", [P, 8], F32) for i in range(4)])
    gv = pg.sb("gv", [P, 512], F32)
    vh = gv
    yv = pg.sb("yv", [P, 4, 128], F32)
    yz = yv
    ysq = pg.sb("ysq", [P, 4, 128], BF16)
    gstd = pg.sb("gstd", [P, 2, 128], F32)
    grs = pg.sb("grs", [P, 2, 128], F32)
    acc_ring = Ring([pg.sb(f"cacc{i}", [P, 4, TB], F32) for i in range(2)])
    for a_ in acc_ring.bufs:
        a_.rq = [Res(f"accq{q}") for q in range(4)]
    pg.push()
    slots = []
    for i_ in range(2):
        slots.append(dict(hn=pg.sb(f"hnb{i_}", [P, KC, TB], BF16), uT=pg.sb(f"uT{i_}", [P, 4, TB], F32),
                          zT=pg.sb(f"zT{i_}", [P, 4, TB], F32), xT=pg.sb(f"xT{i_}", [P, 4, TB], F32),
                          xTb=pg.sb(f"xTb{i_}", [P, 8, TB], BF16),
                          vb=pg.sb(f"vb{i_}", [P, 512], BF16), dt=pg.sb(f"dt{i_}", [P, 8], F32),
                          a=pg.sb(f"a{i_}", [P, 8], F32)))
    xpre = pg.sb("xpre", [P, 8, 3 + TB], F32)
    xpreR = [xpre.r, Res("xpre1")]
    mixT = pg.sb("mixT", [P, KC, TB], BF16)
    vf = pg.sb("vf", [P, 512], F32)
    hT = pg.sb("hT", [P, 512], F32)
    hTpad = pg.sb("hTpad", [P, 8, 128], BF16)
    xdtpad = pg.sb("xdtpad", [P, 8, 128], BF16)
    mset(POOL, hTpad.t[:, :, :], 0.0, [hTpad])
    mset(POOL, xdtpad.t[:, :, :], 0.0, [xdtpad])
    mset(POOL, xpre.t[:, :, 0:3], 0.0, xpreR)

    def padview(b):
        t_ = b.t
        return bass.AP(t_.tensor if hasattr(t_, "tensor") else t_, 0, [[8 * 128, P], [256, 4], [192, 2], [1, 64]])

    rhsA = pg.sb("rhsA", [P, 8, 128], F32)
    expo = pg.sb("expo", [P, 8, 128], F32)
    Lm = expo
    Em = rhsA
    MT = pg.sb("MT", [P, 8, 128], BF16)
    ChT = pg.sb("ChT", [P, 8, 128], BF16)
    xw = pg.sb("xw", [P, 512], BF16)
    Btok = pg.sb("Btok", [P, 256], BF16)
    acol = pg.sb("acol", [P, 8], F32)
    mxt = pg.sb("mxt", [P, 4, 128], F32)

    def softplus_dt(pb_, npart, dt_out, a_out):
        s1 = sm.next()
        tt(DVE, s1.t[0:npart, :], pb_.t[0:npart, 0:8], prow.t[0:npart, R_DTB:R_DTB + 8], ALU.add, [pb_, prow], [s1])
        act(s1.t[0:npart, :], s1.t[0:npart, :], AF.Exp, [s1], [s1])
        act(dt_out.t[0:npart, :], s1.t[0:npart, :], AF.Ln, [s1], [dt_out], bias=1.0)
        tt(DVE, a_out.t[0:npart, :], dt_out.t[0:npart, :], arow.t[0:npart, :], ALU.mult, [dt_out, arow], [a_out])

    mhalf = pg.sb("mhalf", [P, 1], F32)
    mset(POOL, mhalf.t[:, :], -0.5, [mhalf])

    def v_proj(hn, hnres, col0, npart):
        pb_ = psF.next()
        for k in range(KC):
            mm(pb_.t[0:npart, :], hn[:, k, col0:col0 + npart], win.t[:, k, 512:1024], k == 0, k == KC - 1,
               hnres + winres(512, 1024), [pb_], sig=(k == KC - 1))
        cp(ACT, gv.t[0:npart, :], pb_.t[0:npart, :], [pb_], [gv])

    def v_act(npart, vb_out, f32_out=None):
        s1 = sm.next()
        act(gv.t[0:npart, :], gv.t[0:npart, :], GELU, [gv], [gv, s1], accum=s1.t[0:npart, 0:1])
        ts(POOL, s1.t[0:npart, 1:2], s1.t[0:npart, 0:1], -1.0 / 512, None, ALU.mult, None, [s1], [s1])
        act(vb_out.t[0:npart, :], gv.t[0:npart, :], AF.Square, [gv, s1], [vb_out, s1], bias=s1.t[0:npart, 1:2],
            accum=s1.t[0:npart, 2:3])
        ts(POOL, s1.t[0:npart, 3:4], s1.t[0:npart, 2:3], 1.0 / 512, EPS, ALU.mult, ALU.add, [s1], [s1])
        tt(POOL, s1.t[0:npart, 4:5], s1.t[0:npart, 3:4], mhalf.t[0:npart, :], ALU.pow, [s1, mhalf], [s1])
        ts(DVE, vh.t[0:npart, :], gv.t[0:npart, :], s1.t[0:npart, 1:2], s1.t[0:npart, 4:5], ALU.add, ALU.mult,
           [gv, s1], [vh])
        tt(POOL, vh.t[0:npart, :], vh.t[0:npart, :], prow.t[0:npart, R_LNG:R_LNG + 512], ALU.mult, [vh, prow], [vh])
        tt(POOL, vb_out.t[0:npart, :], vh.t[0:npart, :], prow.t[0:npart, R_LNB:R_LNB + 512], ALU.add,
           [vh, prow], [vb_out])
        if f32_out is not None:
            tt(DVE, f32_out.t[0:npart, :], vh.t[0:npart, :], prow.t[0:npart, R_LNB:R_LNB + 512], ALU.add,
               [vh, prow], [f32_out])

    def v_tokmajor(hn, hnres, col0, npart, vb_out, f32_out=None):
        v_proj(hn, hnres, col0, npart)
        v_act(npart, vb_out, f32_out)

    def gate_norm(yv_ap, zap, n, R_extra, mix_dst, mres):
        tt(DVE, yz.t[:, :, 0:n], yv_ap, zap, ALU.mult, [yv] + R_extra, [yv])
        act(ysq.t[:, :, 0:n], yz.t[:, :, 0:n], AF.Square, [yz], [ysq])
        pb_ = psC.next()
        for g in range(2):
            for q in range(2):
                mm(pb_.t[:, g * 128:g * 128 + n], onesb, ysq.t[:, 2 * g + q, 0:n], q == 0, q == 1, [ysq, cb16], [pb_],
                   sig=(g == 1 and q == 1))
        act(gstd.t[:, :, 0:n], pb_.t[:, 0:256].rearrange("p (g t) -> p g t", g=2)[:, :, 0:n], AF.Ln, [pb_], [gstd],
            bias=EPS, scale=1.0 / 256)
        act(grs.t[:, :, 0:n], gstd.t[:, :, 0:n], AF.Exp, [gstd], [grs], scale=-0.5)
        for j in range(4):
            stt(mix_dst[:, 4 + j, 0:n], yz.t[:, j, 0:n], pcol.t[:, PC_SN + j:PC_SN + j + 1], grs.t[:, j // 2, 0:n],
                ALU.mult, ALU.mult, [yz, pcol, grs], mres)

    def conv_tile(pb_, f, n, xpre_ap_taps, xpre_res, wr_pre, out_f32, out_bf, ores):
        wr_pre(pb_)
        acc = acc_ring.next()
        act(acc.t[:, 0, 0:n], pb_.t[:, 0:n], AF.Identity, [pb_, pcol], [acc.rq[0]],
            bias=pcol.t[:, PC_CB + f:PC_CB + f + 1], scale=pcol.t[:, PC_CW + 24 + f:PC_CW + 24 + f + 1])
        for k in range(3):
            stt(acc.t[:, 0, 0:n], xpre_ap_taps(k), pcol.t[:, PC_CW + 8 * k + f:PC_CW + 8 * k + f + 1], acc.t[:, 0, 0:n],
                ALU.mult, ALU.add, [xpre_res, pcol, acc.rq[0]], [acc.rq[0]])
        if out_f32 is not None:
            act(out_f32, acc.t[:, 0, 0:n], AF.Silu, [acc.rq[0]], ores)
            cp(POOL, out_bf, out_f32, ores, ores)
        else:
            act(out_bf, acc.t[:, 0, 0:n], AF.Silu, [acc.rq[0]], ores)

    def wout_residual(mix, mixres, n, dstT, dres_fn, t0):
        for fo in range(KC):
            pb_ = psA.next()
            for k in range(KC):
                mm(pb_.t[:, 0:n], wout.t[:, k, fo * 128:(fo + 1) * 128], mix[:, k, 0:n], k == 0, k == KC - 1,
                   mixres + [woutR[fo // 2]], [pb_], sig=(k == KC - 1))
            tt(DVE, dstT[:, fo, t0:t0 + n], dstT[:, fo, t0:t0 + n], pb_.t[:, 0:n], ALU.add, [pb_] + dres_fn(),
               dres_fn())

    NBLK = SEQ // TB

    def front(b):
        S = slots[b % 2]
        t0 = b * TB
        hres = hpres(t0, TB)
        hn = S["hn"]
        load_tokens_fm(xp, t0, 128, hpT.t, hres, t0, preloaded=(b > 0))
        if b + 1 < NBLK:
            ld(xin_ring.bufs[0].t[:, :], xp[t0 + TB:t0 + 2 * TB, :], [xin_ring.bufs[0]])
        yield
        norm_to(hn.t, [hn], hpT.t, hres, t0, TB, PC_GMIX)
        yield
        def proj4(c0):
            pb_ = psF.next()
            for q in range(4):
                cc = c0 + q * 128
                for k in range(KC):
                    mm(pb_.t[:, q * 128:(q + 1) * 128], win.t[:, k, cc:cc + 128], hn.t[:, k, 0:TB], k == 0, k == KC - 1,
                       winres(cc, cc + 128) + [hn], [pb_], sig=(q == 3 and k == KC - 1))
            return pb_

        pbd = psB.next()
        for k in range(KC):
            mm(pbd.t[:, 0:8], hn.t[:, k, 0:128], win.t[:, k, 2560:2568], k == 0, k == KC - 1,
               [hn] + winres(2560, 2568), [pbd], sig=(k == KC - 1))
        softplus_dt(pbd, 128, S["dt"], S["a"])
        yield
        pb_ = proj4(0)
        cp(ACT, S["uT"].t[:, :, :].rearrange("p f t -> p (f t)"), pb_.t[:, :], [pb_], [S["uT"]])
        yield
        pb_ = proj4(1024)
        cp(ACT, S["zT"].t[:, :, :].rearrange("p f t -> p (f t)"), pb_.t[:, :], [pb_], [S["zT"]])
        yield
        v_proj(hn.t, [hn], 0, 128)
        yield
        S["acc"] = []
        for grp in range(2):
            f0 = grp * 4
            xr = [xpreR[grp]]
            pb_ = proj4(1536 + f0 * 128)
            if b > 0:
                cp(POOL, xpre.t[:, f0:f0 + 4, 0:3], xpre.t[:, f0:f0 + 4, TB:TB + 3], xr, xr)
            cp(ACT, xpre.t[:, f0:f0 + 4, 3:3 + TB], pb_.t[:, :].rearrange("p (f t) -> p f t", f=4), [pb_], xr)
            yield
            acc = acc_ring.next()
            S["acc"].append(acc)
            for q in range(4):
                f = f0 + q
                act(acc.t[:, q, :], xpre.t[:, f, 3:3 + TB], AF.Identity, xr + [pcol], [acc.rq[q]],
                    bias=pcol.t[:, PC_CB + f:PC_CB + f + 1], scale=pcol.t[:, PC_CW + 24 + f:PC_CW + 24 + f + 1])
                for k in range(3):
                    stt(acc.t[:, q, :], xpre.t[:, f, k:k + TB], pcol.t[:, PC_CW + 8 * k + f:PC_CW + 8 * k + f + 1],
                        acc.t[:, q, :], ALU.mult, ALU.add, xr + [pcol, acc.rq[q]], [acc.rq[q]])
                yield

    def burst(b):
        S = slots[b % 2]
        last = (b == NBLK - 1)
        u2 = S["uT"].t[:, :, :].rearrange("p f t -> p (f t)")
        act(u2, u2, GELU, [S["uT"]], [S["uT"]])
        v_act(128, S["vb"], vf if last else None)
        if last:
            st(gv_prompt[:, :], vf.t[:, :], [vf], vf.r)
        z2 = S["zT"].t[:, :, :].rearrange("p f t -> p (f t)")
        act(z2, z2, AF.Silu, [S["zT"]], [S["zT"]])
        a0, a1 = S["acc"]
        act(S["xT"].t[:, :, :], a0.t[:, :, :], AF.Silu, a0.rq, [S["xT"]] + a0.rq)
        act(S["xTb"].t[:, 4:8, :], a1.t[:, :, :], AF.Silu, a1.rq, [S["xTb"]] + a1.rq)
        act(S["xTb"].t[:, 0:4, :], a0.t[:, :, :], AF.Silu, a0.rq, [S["xTb"]] + a0.rq)

    def back(b):
        S = slots[b % 2]
        t0 = b * TB
        hres = hpres(t0, TB)
        uT, zT, xT, xTb, vb_, dt_, a_ = S["uT"], S["zT"], S["xT"], S["xTb"], S["vb"], S["dt"], S["a"]
        cs = slice(0, 128)
        fc = (b == 0)
        pxt = psB.next()
        pxb = pxt.t[:, :].bitcast(BF16)
        for f in range(6):
            tr(pxb[:, f * 128:(f + 1) * 128], xTb.t[:, f, cs], identb, [xTb, cb16], [pxt], sig=(f == 5))
        tt(DVE, rhsA.t[:, :, :], bc(Umat.unsqueeze(1), [P, 8, 128]), bc(a_.t[:, :].unsqueeze(2), [P, 8, 128]),
           ALU.mult, [cst, a_], [rhsA])
        yield
        pac0, pac1 = psK.next(), psK.next()
        mm(pac0.t[:, :], onesf, rhsA.t[:, 0:4, :].rearrange("p h t -> p (h t)"), True, True, [cst, rhsA], [pac0])
        mm(pac1.t[:, :], onesf, rhsA.t[:, 4:8, :].rearrange("p h t -> p (h t)"), True, True, [cst, rhsA], [pac1])
        pcol_ps = psC.next()
        mm(pcol_ps.t[:, 0:8], Umat, a_.t[:, :], True, True, [cst, a_], [pcol_ps])
        cp(DVE, acol.t[:, :], pcol_ps.t[:, 0:8], [pcol_ps], [acol])
        yield
        s2 = sm.next()
        for hh, pac in enumerate((pac0, pac1)):
            hs_ = slice(hh * 4, hh * 4 + 4)
            pv_ = pac.t[:, :].rearrange("p (h t) -> p h t", h=4)
            for q in range(4):
                h = hh * 4 + q
                stt(expo.t[:, h, :], pv_[:, q, :], acol.t[:, h:h + 1], negm, ALU.subtract, ALU.add,
                    [pac, acol, cst], [expo])
            tt(DVE, s2.t[:, hs_], pv_[:, :, 127], acol.t[:, hs_], ALU.subtract, [pac, acol], [s2])
            act(Em.t[:, hs_, :], pv_, AF.Exp, [pac], [Em])
            yield
        act(Lm.t[:, :, :], expo.t[:, :, :], AF.Exp, [expo], [expo])
        act(s2.t[:, :], s2.t[:, :], AF.Exp, [s2], [s2])
        tt(DVE, s2.t[:, :], s2.t[:, :], dt_.t[:, :], ALU.mult, [s2, dt_], [s2])
        yield
        pcb = psC.next()
        for g in range(2):
            mm(pcb.t[:, g * 128:(g + 1) * 128], xTb.t[:, 4 + g, cs], xTb.t[:, 6 + g, cs], True, True, [xTb], [pcb],
               sig=(g == 1))
        pcbv = pcb.t[:, 0:256].rearrange("p (g t) -> p g t", g=2)
        for g in range(2):
            tt(DVE, MT.t[:, 4 * g:4 * g + 4, :], Lm.t[:, 4 * g:4 * g + 4, :],
               bc(pcbv[:, g, :].unsqueeze(1), [P, 4, 128]), ALU.mult, [Lm, pcb], [MT])
        yield
        xtokv = pxb[:, 0:512].rearrange("p (h q) -> p h q", h=8)
        xdv = padview(xdtpad)
        tt(DVE, xdv, xtokv.rearrange("p (j e) q -> p j e q", e=2),
           bc(dt_.t[:, :].unsqueeze(2), [P, 8, 64]).rearrange("p (j e) q -> p j e q", e=2), ALU.mult,
           [pxt, dt_], [xdtpad])
        tt(DVE, xw.t[:, :].rearrange("p (h q) -> p h q", h=8), xtokv, bc(s2.t[:, :].unsqueeze(2), [P, 8, 64]),
           ALU.mult, [pxt, s2], [xw])
        cp(ACT, Btok.t[:, :], pxb[:, 512:768], [pxt], [Btok])
        yield
        pst = psB.next()
        for g in range(2):
            mm(pst.t[:, g * 256:(g + 1) * 256], Btok.t[:, g * 128:(g + 1) * 128], xw.t[:, g * 256:(g + 1) * 256],
               True, True, [Btok, xw], [pst], sig=(g == 1))
        if not fc:
            for g in range(2):
                tt(DVE, ChT.t[:, 4 * g:4 * g + 4, :], bc(xTb.t[:, 6 + g, cs].unsqueeze(1), [P, 4, 128]),
                   Em.t[:, 4 * g:4 * g + 4, :], ALU.mult, [xTb, Em], [ChT])
        yield
        py = psB.next()
        for j in range(4):
            for e in range(2):
                h = 2 * j + e
                mm(py.t[:, j * 128:(j + 1) * 128], xdtpad.t[:, h, :], MT.t[:, h, :], e == 0, fc and e == 1,
                   [xdtpad, MT], [py], sig=(fc and j == 3 and e == 1))
                if not fc:
                    mm(py.t[:, j * 128:(j + 1) * 128], hTpad.t[:, h, :], ChT.t[:, h, :], False, e == 1,
                       [hTpad, ChT], [py], sig=(j == 3 and e == 1))
        yield
        if fc:
            cp(DVE, hT.t[:, :], pst.t[:, :], [pst], [hT])
        else:
            tt(DVE, hT.t[:, :].rearrange("p (h q) -> p h q", h=8), hT.t[:, :].rearrange("p (h q) -> p h q", h=8),
               bc(Em.t[:, :, 127].unsqueeze(2), [P, 8, 64]), ALU.mult, [hT, Em], [hT])
            tt(DVE, hT.t[:, :], hT.t[:, :], pst.t[:, :], ALU.add, [hT, pst], [hT])
        cp(ACT, padview(hTpad), hT.t[:, :].rearrange("p (j e q) -> p j e q", j=4, e=2), [hT], [hTpad])
        yield
        for j in range(4):
            stt(yv.t[:, j, :], xT.t[:, j, cs], pcol.t[:, PC_D + j:PC_D + j + 1], py.t[:, j * 128:(j + 1) * 128],
                ALU.mult, ALU.add, [xT, pcol, py], [yv])
        yield
        gate_norm(yv.t[:, :, :], zT.t[:, :, cs], 128, [zT], mixT.t[:, :, cs], [mixT])
        yield
        pmx = psB.next()
        for h in range(4):
            mm(pmx.t[:, h * 128:(h + 1) * 128], vb_.t[:, h * 128:(h + 1) * 128], WT.t[:, h, :], True, True,
               [vb_, WT], [pmx], sig=(h == 3))
        tt(DVE, mxt.t[:, :, :], pmx.t[:, :].rearrange("p (h t) -> p h t", h=4),
           prow.t[:, R_BS:R_BS + 512].rearrange("p (h t) -> p h t", h=4), ALU.add, [pmx, prow], [mxt])
        tt(DVE, mixT.t[:, 0:4, cs], uT.t[:, :, cs], mxt.t[:, :, :], ALU.mult, [uT, mxt], [mixT])
        yield
        for half in range(2):
            pb_ = psK.next()
            for q in range(4):
                fo = half * 4 + q
                for k in range(KC):
                    mm(pb_.t[:, q * 128:(q + 1) * 128], wout.t[:, k, fo * 128:(fo + 1) * 128], mixT.t[:, k, 0:TB],
                       k == 0, k == KC - 1, [mixT, woutR[fo // 2]], [pb_], sig=(q == 3 and k == KC - 1))
            tt(DVE, hpT.t[:, half * 4:half * 4 + 4, t0:t0 + TB], hpT.t[:, half * 4:half * 4 + 4, t0:t0 + TB],
               pb_.t[:, :].rearrange("p (f t) -> p f t", f=4), ALU.add, [pb_] + hres, hres)
            yield

    def run_gens(gens):
        gens = list(gens)
        while gens:
            for g in list(gens):
                try:
                    next(g)
                except StopIteration:
                    gens.remove(g)

    if STOP == "setup":
        return finalize()
    def front_burst(b):
        for _ in front(b):
            yield
        burst(b)
        yield

    def run_weighted(ga, gb, na, nb):
        live = [ga, gb]
        while any(g is not None for g in live):
            for idx, n in ((0, na), (1, nb)):
                for _ in range(n):
                    if live[idx] is None:
                        break
                    try:
                        next(live[idx])
                    except StopIteration:
                        live[idx] = None

    run_gens([front_burst(0)])
    for b in range(NBLK):
        if b + 1 < NBLK:
            run_weighted(back(b), front_burst(b + 1), 1, 2)
        else:
            run_gens([back(b)])

    ssm_sb = pg.sb("ssm_sb", [P, 4, 128], F32)
    pb = psC.next()
    for j in range(4):
        tr(pb.t[:, j * 128:(j + 1) * 128], hT.t[:, j * 128:(j + 1) * 128], identf, [hT, cst], [pb], sig=(j == 3))
    cp(DVE, ssm_sb.t[:, :, :], pb.t[:, :].rearrange("p (j n) -> p j n", j=4), [pb], [ssm_sb])
    st(ssm_prompt.rearrange("(j p) n -> p j n", p=P), ssm_sb.t[:, :, :], [ssm_sb], ssm_sb.r)
    ctail = pg.sb("ctail", [P, D], F32)
    for half in range(2):
        pb = psC.next()
        for q in range(4):
            f = half * 4 + q
            tr(pb.t[0:3, q * 128:(q + 1) * 128], xpre.t[:, f, TB:TB + 3], identf, xpreR + [cst], [pb], sig=(q == 3))
        cp(DVE, ctail.t[0:3, half * 512:(half + 1) * 512], pb.t[0:3, :], [pb], [ctail])
    st(conv_prompt[:, :], ctail.t[0:3, :], [ctail], ctail.r)

    pg.pop()
    pg.push()
    if STOP == "mixer":
        return finalize()
    load_tokens_fm(xs, 0, NS, hsT.t, hsR, 0)
    hnS = pg.sb("hnS", [P, KC, NS], BF16)
    norm_to(hnS.t, [hnS], hsT.t, hsR, 0, NS, PC_GMIX)
    uS = pg.sb("uS", [P, 4, NS], F32)
    zS = pg.sb("zS", [P, 4, NS], F32)
    xS = pg.sb("xS", [P, 8, NS], F32)
    xSb = pg.sb("xSb", [P, 8, NS], BF16)
    xpS = pg.sb("xpS", [P, 8, NS, 4], F32)
    mixS = pg.sb("mixS", [P, KC, NS], BF16)
    scin = pg.sb("scin", [NS * 3, D], F32)
    ld(scin.t[:, :], sconv[:, :], [scin])
    for half in range(2):
        pb = psC.next()
        for q in range(4):
            f = half * 4 + q
            tr(pb.t[:, q * 128:q * 128 + 48], scin.t[0:48, f * 128:(f + 1) * 128], identf[0:48, 0:48], [scin, cst],
               [pb], sig=(q == 3))
        cp(DVE, xpS.t[:, half * 4:half * 4 + 4, :, 0:3],
           pb.t[:, :].rearrange("p (q t) -> p q t", q=4)[:, :, 0:48].rearrange("p q (b k) -> p q b k", k=3),
           [pb], [xpS])
    for f in range(8):
        c0 = 1536 + f * 128

        def consumeS(pb_, f=f):
            conv_tile(pb_, f, NS, lambda k: xpS.t[:, f, :, k], xpS,
                      lambda pb2: cp(DVE, xpS.t[:, f, :, 3], pb2.t[:, 0:NS], [pb2], [xpS]),
                      xS.t[:, f, :], xSb.t[:, f, :], [xS, xSb])
        proj_fm(win.t, winres(c0, c0 + 128), c0, hnS.t, [hnS], NS, consumeS)
    dtS = pg.sb("dtS", [P, 8], F32)
    aS = pg.sb("aS", [P, 8], F32)
    pb = psB.next()
    for k in range(KC):
        mm(pb.t[0:NS, 0:8], hnS.t[:, k, :], win.t[:, k, 2560:2568], k == 0, k == KC - 1, [hnS] + winres(2560, 2568),
           [pb], sig=(k == KC - 1))
    softplus_dt(pb, NS, dtS, aS)
    eS = pg.sb("eS", [NS, 2, 8, 64], F32)
    dcS = pg.sb("dcS", [P, 8], F32)
    act(dcS.t[0:NS, :], aS.t[0:NS, :], AF.Exp, [aS], [dcS])
    cp(DVE, eS.t[:, 0, :, :], bc(dtS.t[0:NS, :].unsqueeze(2), [NS, 8, 64]), [dtS], [eS])
    cp(DVE, eS.t[:, 1, :, :], bc(dcS.t[0:NS, :].unsqueeze(2), [NS, 8, 64]), [dcS], [eS])
    colS = pg.sb("colS", [P, 2, 4, NS], F32)
    eSf = eS.t[:, :, :, :].rearrange("b a h q -> b (a h q)")
    pb = psC.next()
    for q in range(8):
        tr(pb.t[:, q * NS:(q + 1) * NS], eSf[:, q * 128:(q + 1) * 128], identf[0:NS, 0:NS], [eS, cst], [pb],
           sig=(q == 7))
    cp(DVE, colS.t[:, :, :, :].rearrange("p a j b -> p (a j b)"), pb.t[:, 0:8 * NS], [pb], [colS])
    dtxS = pg.sb("dtxS", [P, 4, NS], F32)
    tt(DVE, dtxS.t[:, :, :], xS.t[:, 0:4, :], colS.t[:, 0, :, :], ALU.mult, [xS, colS], [dtxS])
    BCtok = pg.sb("BCtok", [NS, 512], F32)
    pb = psC.next()
    for q in range(4):
        tr(pb.t[0:NS, q * 128:(q + 1) * 128], xS.t[:, 4 + q, :], identf, [xS, cst], [pb], sig=(q == 3))
    cp(DVE, BCtok.t[:, :], pb.t[0:NS, :], [pb], [BCtok])
    id16 = cst.t[0:NS, C_ID:C_ID + NS]
    yS = pg.sb("yS", [P, 4, NS], F32)
    mset(DVE, yS.t[:, :, :], 0.0, [yS])
    hin_ring = Ring([pg.sb(f"hin{i}", [P, 4, 128], F32) for i in range(3)])
    hout_ring = Ring([pg.sb(f"hout{i}", [P, 4, 128], F32) for i in range(3)])
    t1_ring = Ring([pg.sb(f"t1{i}", [P, 128], F32) for i in range(4)])
    def sfiller():
        for f in range(4):
            proj_fm(win.t, winres(f * 128, f * 128 + 128), f * 128, hnS.t, [hnS], NS,
                    lambda pb_, f=f: act(uS.t[:, f, :], pb_.t[:, 0:NS], GELU, [pb_], [uS]))
        yield
        for f in range(4):
            proj_fm(win.t, winres(1024 + f * 128, 1024 + f * 128 + 128), 1024 + f * 128, hnS.t, [hnS], NS,
                    lambda pb_, f=f: act(zS.t[:, f, :], pb_.t[:, 0:NS], AF.Silu, [pb_], [zS]))
        yield
        scv = sconv.rearrange("(b k) c -> b k c", k=3)
        cov = conv_sample.rearrange("(b k) c -> b k c", k=3)
        pg.dma(SP, cov[:, 0:2, :], scv[:, 1:3, :], [], [], outsem)
        crow = pg.sb("crow", [NS, D], F32)
        for half in range(2):
            pb = psC.next()
            for q in range(4):
                f = half * 4 + q
                tr(pb.t[0:NS, q * 128:(q + 1) * 128], xpS.t[:, f, :, 3], identf, [xpS, cst], [pb], sig=(q == 3))
            cp(DVE, crow.t[:, half * 512:(half + 1) * 512], pb.t[0:NS, :], [pb], [crow])
        st(cov[:, 2, :], crow.t[:, :], [crow], crow.r)
        yield
        vbS = pg.sb("vbS", [P, 512], BF16)
        vfS = pg.sb("vfS", [P, 512], F32)
        v_tokmajor(hnS.t, [hnS], 0, NS, vbS, vfS)
        st(gv_sample[:, :], vfS.t[0:NS, :], [vfS], vfS.r)
        yield
        vTS = pg.sb("vTS", [P, 4, NS], F32)
        pb = psC.next()
        for h in range(4):
            tr(pb.t[:, h * NS:(h + 1) * NS], vfS.t[0:NS, h * 128:(h + 1) * 128], identf[0:NS, 0:NS], [vfS, cst], [pb],
               sig=(h == 3))
        for h in range(4):
            ts(DVE, vTS.t[:, h, :], pb.t[:, h * NS:(h + 1) * NS], pcol.t[:, PC_W00 + h:PC_W00 + h + 1],
               pcol.t[:, PC_B0 + h:PC_B0 + h + 1], ALU.mult, ALU.add, [pb, pcol], [vTS])
        tt(DVE, mixS.t[:, 0:4, :], uS.t[:, :, :], vTS.t[:, :, :], ALU.mult, [uS, vTS], [mixS])
        yield

    sfg = sfiller()
    hins = {}

    def hload(i):
        if i < NS:
            h_ = hin_ring.next()
            ld(h_.t[:, :, :], sssm[i].rearrange("(j p) n -> p j n", p=P), [h_])
            hins[i] = h_

    hload(0)
    hload(1)
    for b_ in range(NS):
        hload(b_ + 2)
        hin = hins.pop(b_)
        pbc = psB.next()
        mm(pbc.t[:, :], bc(id16[:, b_:b_ + 1], [NS, 128]), BCtok.t[:, :], True, True, [cst, BCtok], [pbc])
        hout = hout_ring.next()
        for j in range(4):
            g = j // 2
            t1 = t1_ring.next()
            act(t1.t[:, :], hin.t[:, j, :], AF.Copy, [hin, colS], [t1], scale=colS.t[:, 1, j, b_:b_ + 1])
            stt(hout.t[:, j, :], pbc.t[:, g * 128:(g + 1) * 128], dtxS.t[:, j, b_:b_ + 1], t1.t[:, :], ALU.mult, ALU.add,
                [pbc, dtxS, t1], [hout])
            stt(t1.t[:, :], hout.t[:, j, :], 1.0, pbc.t[:, (2 + g) * 128:(3 + g) * 128], ALU.mult, ALU.mult,
                [hout, pbc], [t1, yS], accum=yS.t[:, j, b_:b_ + 1])
        st(ssm_sample[b_].rearrange("(j p) n -> p j n", p=P), hout.t[:, :, :], [hout], hout.r)
        next(sfg, None)
    for _ in sfg:
        pass
    yvS = pg.sb("yvS", [P, 4, NS], F32)
    for j in range(4):
        stt(yvS.t[:, j, :], xS.t[:, j, :], pcol.t[:, PC_D + j:PC_D + j + 1], yS.t[:, j, :], ALU.mult, ALU.add,
            [xS, pcol, yS], [yvS])
    tt(DVE, yv.t[:, :, 0:NS], yvS.t[:, :, :], yvS.t[:, :, :], ALU.max, [yvS], [yv])
    gate_norm(yv.t[:, :, 0:NS], zS.t[:, :, :], NS, [zS], mixS.t, [mixS])
    wout_residual(mixS.t, [mixS], NS, hsT.t, lambda: hsR, 0)
    pg.pop()
    pg.pop()
    del rings["sq8"]

    if STOP == "mixer_all":
        return finalize()
    pg.push()
    wq = pg.sb("wq", [P, KC, D], BF16)
    wxo = pg.sb("wxo", [P, KC, D], BF16)
    KT = pg.sb("KT", [P, KC, NMEM], BF16)
    Vb = pg.sb("Vb", [P, 2, D], BF16)
    smx = pg.sb("smx", [P, 16], F32)
    mk_rings("x", TBX)

    def load_w(dst, src, nslab=4):
        ldc(dst.t[:, :, :], src.rearrange("(k p) n -> p k n", p=P), [dst])

    pg.push()
    wk = pg.sb("wk", [P, KC, D], BF16)
    wv = pg.sb("wv", [P, KC, D], BF16)
    load_w(wk, w_k)
    load_w(wv, w_v)
    load_w(wq, w_q)
    load_w(wxo, w_xo)
    mem = pg.sb("mem", [P, 2, D], F32)
    mn = pg.sb("mn", [P, 2, D], BF16)
    mnT = pg.sb("mnT", [P, KC, NMEM], BF16)
    ld(mem.t[:, :, :], memp.rearrange("(mt p) c -> p mt c", p=P), [mem])
    for mt in range(2):
        act(mn.t[:, mt, :], mem.t[:, mt, :], AF.Square, [mem], [mn, smx], accum=smx.t[:, mt:mt + 1])
    act(smx.t[:, 2:4], smx.t[:, 0:2], AF.Sqrt, [smx], [smx], bias=EPS, scale=1.0 / D)
    recip(smx.t[:, 4:6], smx.t[:, 2:4], [smx], [smx])
    for mt in range(2):
        ts(DVE, mn.t[:, mt, :], mem.t[:, mt, :], smx.t[:, 4 + mt:5 + mt], None, ALU.mult, None, [mem, smx], [mn])
    for mt in range(2):
        for half in range(2):
            pb = psC.next()
            pbb = pb.t[:, :].bitcast(BF16)
            for q in range(4):
                k = half * 4 + q
                tr(pbb[:, q * 128:(q + 1) * 128], mn.t[:, mt, k * 128:(k + 1) * 128], identb, [mn, cb16], [pb],
                   sig=(q == 3))
            tt(DVE, mnT.t[:, half * 4:half * 4 + 4, mt * 128:(mt + 1) * 128],
               pbb[:, 0:512].rearrange("p (q t) -> p q t", q=4),
               bc(pcol.t[:, PC_GM + half * 4:PC_GM + half * 4 + 4].unsqueeze(2), [P, 4, 128]), ALU.mult,
               [pb, pcol], [mnT])
    for fo in range(KC):
        proj_fm(wk.t, [wk], fo * 128, mnT.t, [mnT], NMEM,
                lambda pb_, fo=fo: cp(ACT, KT.t[:, fo, :], pb_.t[:, 0:NMEM], [pb_], [KT]))
    kv_ring = Ring([pg.sb(f"kvst{i}", [P, D], F32) for i in range(2)])
    for (W_, outd, isV) in ((wk, mk_o, False), (wv, mv_o, True)):
        for mt in range(2):
            kv = kv_ring.next()
            for half in range(2):
                pb = psA.next()
                for k in range(KC):
                    mm(pb.t[:, :], mnT.t[:, k, mt * 128:(mt + 1) * 128], W_.t[:, k, half * 512:(half + 1) * 512],
                       k == 0, k == KC - 1, [mnT, W_], [pb], sig=(k == KC - 1))
                cp(ACT if half == 0 else DVE, kv.t[:, half * 512:(half + 1) * 512], pb.t[:, :], [pb], [kv])
            if isV:
                cp(ACT, Vb.t[:, mt, :], kv.t[:, :], [kv], [Vb])
            st(outd[mt * 128:(mt + 1) * 128, :], kv.t[:, :], [kv], kv.r)
    pg.pop()
    xslots = [dict(hnx=pg.sb(f"hnx{i}", [P, KC, TBX], BF16), qT=pg.sb(f"qT{i}", [P, KC, TBX], BF16)) for i in range(2)]
    oT = pg.sb("oT", [P, KC, TBX], BF16)
    es_ring = Ring([pg.sb(f"es{i}", [P, 2, TBX], BF16) for i in range(3)])
    rs_ring = Ring([pg.sb(f"rsx{i}", [P, TBX], F32) for i in range(2)])

    def xo_residual_g(o_, ores, n, dstT, dres, t0):
        for fo in range(KC):
            pb_ = psA.next()
            for k in range(KC):
                mm(pb_.t[:, 0:n], wxo.t[:, k, fo * 128:(fo + 1) * 128], o_[:, k, 0:n], k == 0, k == KC - 1,
                   ores + [wxo], [pb_], sig=(k == KC - 1))
            tt(DVE, dstT[:, fo, t0:t0 + n], dstT[:, fo, t0:t0 + n], pb_.t[:, 0:n], ALU.add, [pb_] + dres, dres)
            yield

    def xfront(b):
        S = xslots[b % 2]
        t0 = b * TBX
        hres = hpres(t0, TBX)
        hnx, qT = S["hnx"], S["qT"]
        norm_to(hnx.t, [hnx], hpT.t, hres, t0, TBX, PC_GX)
        yield
        for fo in range(KC):
            proj_fm(wq.t, [wq], fo * 128, hnx.t, [hnx], TBX,
                    lambda pb_, fo=fo: act(qT.t[:, fo, :], pb_.t[:, :], AF.Copy, [pb_], [qT], scale=0.0625))
            yield

    def xback(b):
        S = xslots[b % 2]
        t0 = b * TBX
        hres = hpres(t0, TBX)
        qT = S["qT"]
        def scores(h):
            e_ = es_ring.next()
            for mt in range(2):
                pb = psA.next()
                for dc in range(2):
                    mm(pb.t[:, :], KT.t[:, 2 * h + dc, mt * 128:(mt + 1) * 128], qT.t[:, 2 * h + dc, :], dc == 0, dc == 1,
                       [KT, qT], [pb], sig=(dc == 1))
                act(e_.t[:, mt, :], pb.t[:, :], AF.Exp, [pb], [e_])
            return e_

        e_next = scores(0)
        yield
        for h in range(4):
            e_ = e_next
            if h + 1 < 4:
                e_next = scores(h + 1)
                yield
            pbs = psC.next()
            for mt in range(2):
                mm(pbs.t[:, :], onesb, e_.t[:, mt, :], mt == 0, mt == 1, [e_, cb16], [pbs], sig=(mt == 1))
            rs = rs_ring.next()
            act(rs.t[:, :], pbs.t[:, :], AF.Ln, [pbs], [rs])
            act(rs.t[:, :], rs.t[:, :], AF.Exp, [rs], [rs], scale=-1.0)
            yield
            for dc in range(2):
                pb = psA.next()
                for mt in range(2):
                    mm(pb.t[:, :], Vb.t[:, mt, (2 * h + dc) * 128:(2 * h + dc + 1) * 128], e_.t[:, mt, :], mt == 0, mt == 1,
                       [Vb, e_], [pb], sig=(mt == 1))
                tt(DVE, oT.t[:, 2 * h + dc, :], pb.t[:, :], rs.t[:, :], ALU.mult, [pb, rs], [oT])
            yield
        for _ in xo_residual_g(oT.t, [oT], TBX, hpT.t, hres, t0):
            yield

    hnSx = pg.sb("hnSx", [P, KC, NS], BF16)
    qtok = pg.sb("qtok", [NS, D], BF16)
    sS = pg.sb("sS", [P, 2, 4 * NS], F32)
    jx = pg.sb("jx", [P, 256], F32)
    K_ring = Ring([pg.sb(f"Kc{i}", [P, D], F32) for i in range(3)])
    pe_ = pg.sb("pe_", [64, 256], F32)
    pTb = pg.sb("pTb", [P, 2, 64], BF16)
    V_ring = Ring([pg.sb(f"Vc{i}", [P, D], BF16) for i in range(3)])
    oST = pg.sb("oST", [P, KC, NS], BF16)

    def sample_attn():
        norm_to(hnSx.t, [hnSx], hsT.t, hsR, 0, NS, PC_GX)
        yield
        for half in range(2):
            pb = psA.next()
            for k in range(KC):
                mm(pb.t[0:NS, :], hnSx.t[:, k, :], wq.t[:, k, half * 512:(half + 1) * 512], k == 0, k == KC - 1,
                   [hnSx, wq], [pb], sig=(k == KC - 1))
            act(qtok.t[:, half * 512:(half + 1) * 512], pb.t[0:NS, :], AF.Copy, [pb], [qtok], scale=0.0625)
            yield
        mset(DVE, sS.t[:, :, :], 0.0, [sS])
        kbufs = {}

        def kload(i):
            if i < 2 * NS:
                kb_ = K_ring.next()
                ld(kb_.t[:, :], ck[i // 2][(i % 2) * 128:(i % 2 + 1) * 128, :], [kb_])
                kbufs[i] = kb_

        kload(0)
        kload(1)
        for b_ in range(NS):
            for mt in range(2):
                i = 2 * b_ + mt
                kload(i + 2)
                Kb = kbufs.pop(i)
                if mt == 0:
                    yield
                if mt == 0:
                    pq = [psA.next(), psA.next()]
                    for half in range(2):
                        mm(pq[half].t[:, :], bc(identb[0:NS, b_:b_ + 1], [NS, 128]), qtok.t[:, half * 512:(half + 1) * 512],
                           True, True, [cb16, qtok], [pq[half]])
                for h in range(4):
                    stt(jx.t[:, :], Kb.t[:, h * 256:(h + 1) * 256], 1.0, pq[h // 2].t[:, (h % 2) * 256:(h % 2 + 1) * 256],
                        ALU.mult, ALU.mult, [Kb, pq[h // 2]], [jx, sS], accum=sS.t[:, mt, b_ * 4 + h:b_ * 4 + h + 1])
            yield
        pt = psC.next()
        for mt in range(2):
            tr(pt.t[0:64, mt * 128:(mt + 1) * 128], sS.t[:, mt, :], identf, [sS, cst], [pt], sig=(mt == 1))
        pg.op(DVE, lambda: nc.vector.reduce_max(out=smx.t[0:64, 8:9], in_=pt.t[0:64, 0:256], axis=AX.X), [pt.r], [smx.r])
        ts(DVE, smx.t[0:64, 9:10], smx.t[0:64, 8:9], -1.0, None, ALU.mult, None, [smx], [smx])
        act(pe_.t[:, :], pt.t[0:64, 0:256], AF.Exp, [pt, smx], [pe_, smx], bias=smx.t[0:64, 9:10], accum=smx.t[0:64, 10:11])
        recip(smx.t[0:64, 11:12], smx.t[0:64, 10:11], [smx], [smx])
        ts(DVE, pe_.t[:, :], pe_.t[:, :], smx.t[0:64, 11:12], None, ALU.mult, None, [pe_, smx], [pe_])
        yield
        pt2 = psC.next()
        for mt in range(2):
            tr(pt2.t[:, mt * 64:(mt + 1) * 64], pe_.t[0:64, mt * 128:(mt + 1) * 128], identf[0:64, 0:64], [pe_, cst], [pt2],
               sig=(mt == 1))
        cp(DVE, pTb.t[:, :, :], pt2.t[:, 0:128].rearrange("p (m c) -> p m c", m=2), [pt2], [pTb])
        yield
        poS = psB.next()
        vbufs = {}

        def vload(i):
            if i < 2 * NS:
                vb_ = V_ring.next()
                ldc(vb_.t[:, :], cv[i // 2][(i % 2) * 128:(i % 2 + 1) * 128, :], [vb_])
                vbufs[i] = vb_

        vload(0)
        vload(1)
        for b_ in range(NS):
            for mt in range(2):
                i = 2 * b_ + mt
                vload(i + 2)
                Vc = vbufs.pop(i)
                yield
                for c in range(KC):
                    h = c // 2
                    col = mt * KC * NS + c * NS + b_
                    mm(poS.t[:, col:col + 1], Vc.t[:, c * 128:(c + 1) * 128], pTb.t[:, mt, b_ * 4 + h:b_ * 4 + h + 1],
                       True, True, [Vc, pTb], [poS], sig=(c == KC - 1))
        yield
        cp(DVE, jx.t[:, 0:KC * NS], poS.t[:, 0:KC * NS], [poS], [jx])
        tt(DVE, oST.t[:, :, :], jx.t[:, 0:KC * NS].rearrange("p (c b) -> p c b", c=KC),
           poS.t[:, KC * NS:2 * KC * NS].rearrange("p (c b) -> p c b", c=KC), ALU.add, [poS, jx], [oST])
        for _ in xo_residual_g(oST.t, [oST], NS, hsT.t, hsR, 0):
            yield

    def run_with_filler(main, filler):
        main = list(main)
        while main:
            for g in list(main):
                try:
                    next(g)
                except StopIteration:
                    main.remove(g)
            if filler[0] is not None:
                try:
                    next(filler[0])
                except StopIteration:
                    filler[0] = None

    NXB = SEQ // TBX
    sfill = [sample_attn()]
    run_with_filler([xfront(0)], sfill)
    for b in range(NXB):
        gs = [xback(b)]
        if b + 1 < NXB:
            gs.append(xfront(b + 1))
        run_with_filler(gs, sfill)
    if sfill[0] is not None:
        run_gens([sfill[0]])
    pg.pop()
    if STOP == "xattn_s":
        return finalize()

    pg.push()
    mk_rings("f", TBX)
    hnF = pg.sb("hnF", [P, KC, SEQ], BF16)
    hnFR = [Res(f"hnF{i}") for i in range(SEQ // TBX)]
    hnSF = pg.sb("hnSF", [P, KC, NS], BF16)
    for b in range(SEQ // TBX):
        norm_to(hnF.t, [hnFR[b]], hpT.t, hpres(b * TBX, TBX), b * TBX, TBX, PC_GF, d0=b * TBX)
    norm_to(hnSF.t, [hnSF], hsT.t, hsR, 0, NS, PC_GF)
    NH = NJ // 2
    actT = pg.sb("actT", [P, NH, SEQ], BF16)
    actR = [Res(f"act{i}") for i in range(SEQ // TBX)]
    actS = pg.sb("actS", [P, NH, NS], BF16)
    wd = pg.sb("wd", [P, NH, D], BF16)
    wg_ring = Ring([pg.sb(f"wg{i}", [P, KC, 128], BF16) for i in range(2)])
    wu_ring = Ring([pg.sb(f"wu{i}", [P, KC, 128], BF16) for i in range(2)])
    sg_ring = Ring([pg.sb(f"sg{i}", [P, TBX], F32) for i in range(2)])
    w_gate_v = w_gate.rearrange("(k p) n -> p k n", p=P)
    w_up_v = w_up.rearrange("(k p) n -> p k n", p=P)
    nfrow = pg.sb("nfrow", [P, D], F32)
    ld(nfrow.t[:, :], nf_d[:, :], [nfrow])
    yt_ring = Ring([pg.sb(f"yt{i}", [P, D], F32) for i in range(2)])
    sq2 = pg.sb("sq2", [P, 512], F32)
    fs_ring = Ring([pg.sb(f"fs{i}", [P, 8], F32) for i in range(2)])

    def final_norm(srcT, sres, c0, n, out_ap):
        yt = yt_ring.next()
        fs = fs_ring.next()
        for half in range(2):
            pb_ = psC.next()
            for q in range(4):
                tr(pb_.t[0:n, q * 128:(q + 1) * 128], srcT[:, half * 4 + q, c0:c0 + n], identf, sres + [cst], [pb_],
                   sig=(q == 3))
            cp(DVE, yt.t[0:n, half * 512:(half + 1) * 512], pb_.t[0:n, :], [pb_], [yt])
            act(sq2.t[0:n, :], pb_.t[0:n, :], AF.Square, [pb_], [sq2, fs], accum=fs.t[0:n, half:half + 1])
        tt(DVE, fs.t[0:n, 2:3], fs.t[0:n, 0:1], fs.t[0:n, 1:2], ALU.add, [fs], [fs])
        act(fs.t[0:n, 3:4], fs.t[0:n, 2:3], AF.Sqrt, [fs], [fs], bias=EPS, scale=1.0 / D)
        recip(fs.t[0:n, 4:5], fs.t[0:n, 3:4], [fs], [fs])
        stt(yt.t[0:n, :], yt.t[0:n, :], fs.t[0:n, 4:5], nfrow.t[0:n, :], ALU.mult, ALU.mult, [yt, fs, nfrow], [yt])
        st(out_ap, yt.t[0:n, :], [yt], yt.r)

    for hf in range(2):
        j0 = hf * NH
        for jj in range(NH):
            j = j0 + jj
            g_ = wg_ring.next()
            u_ = wu_ring.next()
            ldc(g_.t[:, :, :], w_gate_v[:, :, j * 128:(j + 1) * 128], [g_])
            ldc(u_.t[:, :, :], w_up_v[:, :, j * 128:(j + 1) * 128], [u_])
            if jj == 0:
                wdv = w_down[j0 * 128:(j0 + NH) * 128, :].rearrange("(j p) n -> p j n", p=P)
                ldc(wd.t[:, :, :], wdv[:, :, :], [wd])
            for tb in range(SEQ // TBX):
                tsl = slice(tb * TBX, (tb + 1) * TBX)
                pg_ = psA.next()
                for k in range(KC):
                    mm(pg_.t[:, :], g_.t[:, k, :], hnF.t[:, k, tsl], k == 0, k == KC - 1, [g_, hnFR[tb]], [pg_],
                       sig=(k == KC - 1))
                pu_ = psA.next()
                for k in range(KC):
                    mm(pu_.t[:, :], u_.t[:, k, :], hnF.t[:, k, tsl], k == 0, k == KC - 1, [u_, hnFR[tb]], [pu_],
                       sig=(k == KC - 1))
                sg = sg_ring.next()
                act(sg.t[:, :], pg_.t[:, :], AF.Silu, [pg_], [sg])
                tt(DVE, actT.t[:, jj, tsl], sg.t[:, :], pu_.t[:, :], ALU.mult, [sg, pu_], [actR[tb]])
            pS = psB.next()
            for k in range(KC):
                mm(pS.t[:, 0:NS], g_.t[:, k, :], hnSF.t[:, k, :], k == 0, k == KC - 1, [g_, hnSF], [pS], sig=False)
            for k in range(KC):
                mm(pS.t[:, NS:2 * NS], u_.t[:, k, :], hnSF.t[:, k, :], k == 0, k == KC - 1, [u_, hnSF], [pS],
                   sig=(k == KC - 1))
            sg = sg_ring.next()
            act(sg.t[:, 0:NS], pS.t[:, 0:NS], AF.Silu, [pS], [sg])
            tt(DVE, actS.t[:, jj, :], sg.t[:, 0:NS], pS.t[:, NS:2 * NS], ALU.mult, [sg, pS], [actS])
        for tb in range(SEQ // TBX):
            tsl = slice(tb * TBX, (tb + 1) * TBX)
            hres = hpres(tb * TBX, TBX)
            for fo in range(KC):
                pb = psA.next()
                for jj in range(NH):
                    mm(pb.t[:, :], wd.t[:, jj, fo * 128:(fo + 1) * 128], actT.t[:, jj, tsl], jj == 0, jj == NH - 1,
                       [wd, actR[tb]], [pb], sig=(jj == NH - 1))
                tt(DVE, hpT.t[:, fo, tsl], hpT.t[:, fo, tsl], pb.t[:, :], ALU.add, [pb] + hres, hres)
            if hf == 1:
                for i in range(tb * (TBX // 128), (tb + 1) * (TBX // 128)):
                    final_norm(hpT.t, hpres(i * 128, 128), i * 128, 128, y_prompt[i * 128:(i + 1) * 128, :])
        for fo in range(KC):
            pb = psB.next()
            for jj in range(NH):
                mm(pb.t[:, 0:NS], wd.t[:, jj, fo * 128:(fo + 1) * 128], actS.t[:, jj, :], jj == 0, jj == NH - 1,
                   [wd, actS], [pb], sig=(jj == NH - 1))
            tt(DVE, hsT.t[:, fo, :], hsT.t[:, fo, :], pb.t[:, 0:NS], ALU.add, [pb] + hsR, hsR)
        if hf == 1:
            final_norm(hsT.t, hsR, 0, NS, y_sample[:, :])
    pg.pop()
    if STOP == "ffn":
        return finalize()

    return finalize()


_NC_CACHE = {}


def kernel(**inp):
    inp = {k: np.asarray(v) for k, v in inp.items()}
    if "nc" not in _NC_CACHE:
        _NC_CACHE["nc"] = build_program()
    nc = _NC_CACHE["nc"]
    prow, pcol, nf = _tables(inp)
    consts = _consts()
    shared = {
        "w_in": np.ascontiguousarray(inp["w_in"][0]), "w_out": np.ascontiguousarray(inp["w_out"][0]),
        "w_q": np.ascontiguousarray(inp["w_q"][0]), "w_k": np.ascontiguousarray(inp["w_k"][0]),
        "w_v": np.ascontiguousarray(inp["w_v"][0]), "w_xo": np.ascontiguousarray(inp["w_xo"][0]),
        "w_gate": np.ascontiguousarray(inp["w_gate"][0]), "w_up": np.ascontiguousarray(inp["w_up"][0]),
        "w_down": np.ascontiguousarray(inp["w_down"][0]),
        "gws": np.ascontiguousarray(inp["gmlp_ws"][0].reshape(512, 128)),
        "consts": consts, "prow": prow, "pcol": pcol, "nfrow": nf,
    }
    in_maps = []
    for c in range(NCORES):
        s = slice(c * NS, (c + 1) * NS)
        m = dict(shared)
        m["xp"] = np.ascontiguousarray(inp["x_prompt"][c])
        m["memp"] = np.ascontiguousarray(inp["mem_prompt"][c])
        m["xs"] = np.ascontiguousarray(inp["x_sample"][s, 0, :])
        m["sssm"] = np.ascontiguousarray(inp["state_ssm"][0, s].reshape(NS, 512, 128))
        m["sconv"] = np.ascontiguousarray(inp["state_conv"][0, s].reshape(NS * 3, D))
        m["ck"] = np.ascontiguousarray(inp["cache_mem_k"][0, s].reshape(NS, NMEM, D))
        m["cv"] = np.ascontiguousarray(inp["cache_mem_v"][0, s].reshape(NS, NMEM, D))
        in_maps.append(m)
    res = run_bass_kernel_spmd(nc, in_maps, core_ids=list(range(NCORES)))
    R = res.results
    _NC_CACHE["last"] = R

    def cat(name, shape_per):
        return np.stack([np.asarray(R[c][name], np.float32).reshape(shape_per) for c in range(NCORES)], 0)

    y_prompt = cat("y_prompt", (SEQ, D))
    y_sample = cat("y_sample", (NS, D)).reshape(NCORES * NS, 1, D)
    ssm_p = cat("ssm_prompt", (8, 64, 128))[None]
    conv_p = cat("conv_prompt", (3, D))[None]
    gv_p = cat("gv_prompt", (128, 512))[None]
    mk = cat("mk_o", (NMEM, 4, 256))[None]
    mv = cat("mv_o", (NMEM, 4, 256))[None]
    ssm_s = cat("ssm_sample", (NS, 8, 64, 128)).reshape(1, NCORES * NS, 8, 64, 128)
    conv_s = cat("conv_sample", (NS, 3, D)).reshape(1, NCORES * NS, 3, D)
    gv_s = cat("gv_sample", (NS, 512)).reshape(1, NCORES * NS, 1, 512)
    return (y_prompt, y_sample, ssm_p, conv_p, gv_p, mk, mv, ssm_s, conv_s, gv_s)
```

```python
import numpy as np
from contextlib import ExitStack
import concourse.bass as bass
import concourse.mybir as mybir
from concourse.bass_utils import run_bass_kernel_spmd

F32 = mybir.dt.float32
BF16 = mybir.dt.bfloat16
AF = mybir.ActivationFunctionType
ALU = mybir.AluOpType
AX = mybir.AxisListType

NCORES = 8
P = 128
D = 1024
KC = 8
SEQ = 2048
DIN = 2568
DFF = 2816
NJ = DFF // 128
NS = 16
NMEM = 256
EPS = 1e-6
TB = 128
TBX = 512
GELU = AF.Gelu_apprx_tanh

DEBUG = False
STOP = None


class StopBuild(Exception):
    pass


def ckpt(name):
    if STOP == name:
        raise StopBuild()


class Tok:
    __slots__ = ("sem", "val")

    def __init__(self, sem, val):
        self.sem = sem
        self.val = val


class Res:
    __slots__ = ("w", "r", "dsem", "dcnt", "name", "excl")

    def __init__(self, name=""):
        self.excl = False
        self.w = None
        self.r = []
        self.dsem = None
        self.dcnt = 0
        self.name = name


class Buf:
    def __init__(self, t, name):
        self.t = t
        self.r = Res(name)


class Eng:
    def __init__(self, pg, name, h):
        self.pg = pg
        self.name = name
        self.h = h
        self.sem = pg.newsem("e_" + name)
        pg.sem2eng[id(self.sem)] = self
        self.cnt = 0
        self.waited = {}
        self.nins = 0

    def wait(self, tok):
        if tok is None:
            return
        k = id(tok.sem)
        if self.waited.get(k, 0) >= tok.val:
            return
        prod = self.pg.sem2eng.get(k)
        if prod is not None and tok.val > prod.cnt:
            raise RuntimeError(f"wait on future token of {prod.name} from {self.name}: potential deadlock")
        self.h.wait_ge(tok.sem, tok.val)
        self.waited[k] = tok.val


class PG:
    def __init__(self, nc, es):
        self.nc = nc
        self.es = es
        self.nsem = 0
        self.sem2eng = {}
        self.PE = Eng(self, "pe", nc.tensor)
        self.ACT = Eng(self, "act", nc.scalar)
        self.DVE = Eng(self, "dve", nc.vector)
        self.POOL = Eng(self, "pool", nc.gpsimd)
        self.SP = Eng(self, "sp", nc.sync)
        self.dma_res = []
        self.nbuf = 0
        self.scopes = [es]

    def push(self):
        e = ExitStack()
        self.scopes.append(e)

    def pop(self):
        self.barrier()
        self.scopes.pop().close()

    def barrier(self):
        engs = [self.PE, self.ACT, self.DVE, self.POOL, self.SP]
        toks = [Tok(X.sem, X.cnt) for X in engs if X.cnt > 0]
        toks += [Tok(R.dsem, R.dcnt) for R in self.dma_res]
        for E in engs:
            for t in toks:
                E.wait(t)

    def newsem(self, name):
        self.nsem += 1
        return self.es.enter_context(self.nc.semaphore(name))

    def sb(self, name, shape, dt):
        t = self.scopes[-1].enter_context(self.nc.sbuf_tensor("s_" + name, list(shape), dt))
        if DEBUG:
            nb = int(np.prod(shape[1:])) * (2 if dt == BF16 else 4)
            self.nbuf += nb
            print("SB", name, shape, nb, "cum", self.nbuf, "depth", len(self.scopes))
        return Buf(t, name)

    def ps(self, name, shape, dt):
        t = self.es.enter_context(self.nc.psum_tensor("p_" + name, list(shape), dt))
        return Buf(t, name)

    @staticmethod
    def _addread(res, tok):
        for i, t in enumerate(res.r):
            if t.sem is tok.sem:
                res.r[i] = tok
                return
        res.r.append(tok)

    def op(self, E, emit, reads=(), writes=(), signal=True):
        ex = [r for r in reads if r.excl]
        if ex:
            reads = [r for r in reads if not r.excl]
            writes = list(writes) + [r for r in ex if r not in writes]
        for r in reads:
            if r.w is not None:
                E.wait(r.w)
        for w in writes:
            if w.w is not None and w.w.sem is not E.sem:
                E.wait(w.w)
            for t in w.r:
                if t.sem is not E.sem:
                    E.wait(t)
        ins = emit()
        E.nins += 1
        if signal:
            E.cnt += 1
            ins.then_inc(E.sem, 1)
            tok = Tok(E.sem, E.cnt)
        else:
            tok = Tok(E.sem, E.cnt + 1)
        for w in writes:
            w.w = tok
            w.r = []
        for r in reads:
            self._addread(r, tok)
        return tok

    def dma(self, Q, out, in_, reads=(), writes=(), semres=None):
        for r in reads:
            if r.w is not None:
                Q.wait(r.w)
        for w in writes:
            if w.w is not None:
                Q.wait(w.w)
            for t in w.r:
                Q.wait(t)
        R = semres if semres is not None else (writes[0] if writes else reads[0])
        if R.dsem is None:
            R.dsem = self.newsem("d_" + R.name)
            self.dma_res.append(R)
        R.dcnt += 16
        Q.h.dma_start(out=out, in_=in_).then_inc(R.dsem, 16)
        Q.nins += 1
        tok = Tok(R.dsem, R.dcnt)
        for w in writes:
            w.w = tok
            w.r = []
        for r in reads:
            self._addread(r, tok)
        return tok

    def finish(self, E):
        for R in self.dma_res:
            E.wait(Tok(R.dsem, R.dcnt))
        for X in (self.PE, self.ACT, self.DVE, self.POOL):
            if X is not E and X.cnt > 0:
                E.wait(Tok(X.sem, X.cnt))


class Ring:
    def __init__(self, bufs):
        self.bufs = bufs
        self.i = 0

    def next(self):
        b = self.bufs[self.i % len(self.bufs)]
        self.i += 1
        return b


C_ID, C_U, C_NEG, C_ONE = 0, 128, 256, 384
CW = 512


def _consts():
    c = np.zeros((P, CW), np.float32)
    c[:, C_ID:C_ID + 128] = np.eye(128, dtype=np.float32)
    s = np.arange(128)[:, None]
    t = np.arange(128)[None, :]
    c[:, C_U:C_U + 128] = (s <= t).astype(np.float32)
    c[:, C_NEG:C_NEG + 128] = np.where(s <= t, 0.0, -1e30).astype(np.float32)
    c[:, C_ONE:C_ONE + 128] = 1.0
    return c


R_LNG, R_LNB, R_DTB, R_ALOG, R_BS = 0, 512, 1024, 1032, 1040
RW = 1040 + 512
PC_GMIX, PC_GX, PC_GF, PC_GM, PC_CW, PC_CB, PC_SN, PC_D, PC_W00, PC_B0 = 0, 8, 16, 24, 32, 64, 72, 76, 80, 84
PCW = 88


def _tables(inp):
    row = np.concatenate([inp["gmlp_ln_g"][0], inp["gmlp_ln_b"][0], inp["dt_bias"][0], inp["a_log"][0],
                          inp["gmlp_bs"][0].reshape(-1)]).astype(np.float32)
    prow = np.ascontiguousarray(np.broadcast_to(row[None, :], (P, RW)))
    pc = np.zeros((P, PCW), np.float32)

    def col(v):
        return np.asarray(v, np.float32).reshape(-1, P).T

    pc[:, PC_GMIX:PC_GMIX + 8] = col(inp["norm_mix"][0])
    pc[:, PC_GX:PC_GX + 8] = col(inp["norm_xattn"][0])
    pc[:, PC_GF:PC_GF + 8] = col(inp["norm_ffn"][0])
    pc[:, PC_GM:PC_GM + 8] = col(inp["norm_mem"][0])
    for k in range(4):
        pc[:, PC_CW + 8 * k:PC_CW + 8 * k + 8] = col(inp["conv_w"][0, k])
    pc[:, PC_CB:PC_CB + 8] = col(inp["conv_b"][0])
    pc[:, PC_SN:PC_SN + 4] = col(inp["ssd_norm"][0])
    pc[:, PC_D:PC_D + 4] = col(np.repeat(inp["d_skip"][0], 64))
    pc[:, PC_W00:PC_W00 + 4] = col(np.repeat(inp["gmlp_ws"][0, :, 0, 0], 128))
    pc[:, PC_B0:PC_B0 + 4] = col(np.repeat(inp["gmlp_bs"][0, :, 0], 128))
    nf = np.ascontiguousarray(np.broadcast_to(inp["norm_final"][None, :].astype(np.float32), (P, D)))
    return prow, pc, nf


def build_program():
    st_ = {}
    try:
        return _build(st_)
    except StopBuild:
        return st_["finalize"]()


def _build(st_):
    nc = bass.Bass("TRN2", target_bir_lowering=False)
    es = ExitStack()
    pg = PG(nc, es)
    PE, ACT, DVE, POOL, SP = pg.PE, pg.ACT, pg.DVE, pg.POOL, pg.SP

    def din(name, shape, dt=F32):
        return nc.dram_tensor(name, list(shape), dt, kind="ExternalInput").ap()

    def dout(name, shape, dt=F32):
        return nc.dram_tensor(name, list(shape), dt, kind="ExternalOutput").ap()

    xp = din("xp", [SEQ, D])
    memp = din("memp", [NMEM, D])
    xs = din("xs", [NS, D])
    sssm = din("sssm", [NS, 512, 128])
    sconv = din("sconv", [NS * 3, D])
    ck = din("ck", [NS, NMEM, D])
    cv = din("cv", [NS, NMEM, D])
    w_in = din("w_in", [D, DIN])
    w_out = din("w_out", [D, D])
    w_q = din("w_q", [D, D])
    w_k = din("w_k", [D, D])
    w_v = din("w_v", [D, D])
    w_xo = din("w_xo", [D, D])
    w_gate = din("w_gate", [D, DFF])
    w_up = din("w_up", [D, DFF])
    w_down = din("w_down", [DFF, D])
    gws = din("gws", [4 * 128, 128])
    consts_d = din("consts", [P, CW])
    prow_d = din("prow", [P, RW])
    pcol_d = din("pcol", [P, PCW])
    nf_d = din("nfrow", [P, D])

    y_prompt = dout("y_prompt", [SEQ, D])
    y_sample = dout("y_sample", [NS, D])
    ssm_prompt = dout("ssm_prompt", [512, 128])
    conv_prompt = dout("conv_prompt", [3, D])
    gv_prompt = dout("gv_prompt", [128, 512])
    mk_o = dout("mk_o", [NMEM, D])
    mv_o = dout("mv_o", [NMEM, D])
    ssm_sample = dout("ssm_sample", [NS, 512, 128])
    conv_sample = dout("conv_sample", [NS * 3, D])
    gv_sample = dout("gv_sample", [NS, 512])
    dbg_o = dout("dbg", [P, KC * SEQ]) if DEBUG else None
    dbgs_o = dout("dbgs", [P, KC * NS]) if DEBUG else None
    dbgs2_o = dout("dbg2", [P, 16384]) if DEBUG else None

    def rl(xs_):
        out = []
        for x in xs_:
            out.append(x.r if isinstance(x, Buf) else x)
        return out

    def mm(out, lhsT, rhs, start, stop, R, W, sig=True):
        return pg.op(PE, lambda: nc.tensor.matmul(out, lhsT=lhsT, rhs=rhs, start=start, stop=stop),
                     rl(R), rl(W), sig)

    def tr(out, in_, ident, R, W, sig=True):
        return pg.op(PE, lambda: nc.tensor.transpose(out, in_, ident), rl(R), rl(W), sig)

    def act(out, in_, func, R, W, bias=None, scale=None, accum=None):
        kw = {}
        if bias is not None:
            kw["bias"] = bias
        if scale is not None:
            kw["scale"] = scale
        if accum is not None:
            kw["accum_out"] = accum
        return pg.op(ACT, lambda: nc.scalar.activation(out=out, in_=in_, func=func, **kw), rl(R), rl(W))

    def eh(E):
        return nc.vector if E is DVE else nc.gpsimd

    def tt(E, out, in0, in1, op, R, W):
        return pg.op(E, lambda: eh(E).tensor_tensor(out=out, in0=in0, in1=in1, op=op), rl(R), rl(W))

    def ts(E, out, in0, s1, s2, op0, op1, R, W):
        if op1 is None:
            return pg.op(E, lambda: eh(E).tensor_scalar(out=out, in0=in0, scalar1=s1, scalar2=None, op0=op0),
                         rl(R), rl(W))
        return pg.op(E, lambda: eh(E).tensor_scalar(out=out, in0=in0, scalar1=s1, scalar2=s2, op0=op0, op1=op1),
                     rl(R), rl(W))

    def stt(out, in0, scalar, in1, op0, op1, R, W, accum=None):
        if accum is None:
            return pg.op(DVE, lambda: nc.vector.scalar_tensor_tensor(out=out, in0=in0, scalar=scalar, in1=in1,
                                                                     op0=op0, op1=op1), rl(R), rl(W))
        return pg.op(DVE, lambda: nc.vector.scalar_tensor_tensor(out=out, in0=in0, scalar=scalar, in1=in1,
                                                                 op0=op0, op1=op1, accum_out=accum), rl(R), rl(W))

    def cp(E, out, in_, R, W):
        if E is ACT:
            return pg.op(ACT, lambda: nc.scalar.copy(out=out, in_=in_), rl(R), rl(W))
        return pg.op(E, lambda: eh(E).tensor_copy(out=out, in_=in_), rl(R), rl(W))

    def recip(out, in_, R, W):
        return pg.op(DVE, lambda: nc.vector.reciprocal(out=out, in_=in_), rl(R), rl(W))

    def mset(E, ap, val, W):
        return pg.op(E, lambda: eh(E).memset(ap, val), [], rl(W))

    def ld(out, in_, W, R=(), semres=None):
        return pg.dma(SP, out, in_, rl(R), rl(W), semres)

    def ldc(out, in_, W, R=(), semres=None):
        return pg.dma(POOL, out, in_, rl(R), rl(W), semres)

    def st(out, in_, R, semres):
        return pg.dma(SP, out, in_, rl(R), [], semres)

    def bc(ap, shape):
        return ap.broadcast_to(list(shape))

    setup = Res("setup")
    outsem = Res("outs")
    cst = pg.sb("cst", [P, CW], F32)
    prow = pg.sb("prow", [P, RW], F32)
    pcol = pg.sb("pcol", [P, PCW], F32)
    ld(cst.t[:, :], consts_d[:, :], [cst])
    ld(prow.t[:, :], prow_d[:, :], [prow])
    ld(pcol.t[:, :], pcol_d[:, :], [pcol])
    identf = cst.t[:, C_ID:C_ID + 128]
    Umat = cst.t[:, C_U:C_U + 128]
    negm = cst.t[:, C_NEG:C_NEG + 128]
    onesf = cst.t[:, C_ONE:C_ONE + 128]
    cb16 = pg.sb("cb16", [P, 256], BF16)
    cp(DVE, cb16.t[:, 0:128], identf, [cst], [cb16])
    cp(DVE, cb16.t[:, 128:256], onesf, [cst], [cb16])
    identb = cb16.t[:, 0:128]
    onesb = cb16.t[:, 128:256]
    arow = pg.sb("arow", [P, 8], F32)
    act(arow.t[:, :], prow.t[:, R_ALOG:R_ALOG + 8], AF.Exp, [prow], [arow])
    ts(DVE, arow.t[:, :], arow.t[:, :], -1.0, None, ALU.mult, None, [arow], [arow])

    PSB = [pg.ps(f"psb{i}", [P, 512], F32) for i in range(8)]
    for b_ in PSB:
        b_.r.excl = True
    psA = Ring(PSB[0:4])
    psF = Ring(PSB[0:2])
    psK = Ring(PSB[2:4])
    psB = Ring(PSB[4:6])
    psC = Ring(PSB[6:8])

    hpT = pg.sb("hpT", [P, KC, SEQ], F32)
    hpR = [Res(f"hp{i}") for i in range(SEQ // TB)]

    def hpres(t0, n):
        return [hpR[i] for i in range(t0 // TB, (t0 + n + TB - 1) // TB)]

    hsT = pg.sb("hsT", [P, KC, NS], F32)
    hsR = [Res("hs")]

    def finalize():
        if DEBUG:
            pg.dma(SP, dbg_o[:, :], hpT.t[:, :, :].rearrange("p k t -> p (k t)"), hpR, [], outsem)
            pg.dma(SP, dbgs_o[:, :], hsT.t[:, :, :].rearrange("p k t -> p (k t)"), hsR, [], outsem)
            if STOP is not None and STOP.startswith("block") and "dumps" in st_:
                off = 0
                for nm, b_, n_ in st_["dumps"]():
                    ap_ = b_.t[:, :] if len(b_.t.shape) == 2 else (b_.t[:, :, :].rearrange("p a b -> p (a b)"))
                    pg.dma(POOL if b_.t.dtype == BF16 else SP, dbgs2_o[:, off:off + n_], ap_, [b_.r], [], outsem)
                    print("DUMP", nm, off, n_)
                    off += n_
        while len(pg.scopes) > 1:
            pg.pop()
        pg.finish(SP)
        es.close()
        return nc
    st_["finalize"] = finalize

    WT = pg.sb("WT", [P, 4, 128], BF16)
    pg.push()
    wsraw = pg.sb("wsraw", [P, 4, 128], F32)
    ld(wsraw.t[:, :, :], gws.rearrange("(h t) s -> t h s", t=128), [wsraw])
    pb = psC.next()
    for h in range(4):
        tr(pb.t[:, h * 128:(h + 1) * 128], wsraw.t[:, h, :], identf, [wsraw, cst], [pb], sig=(h == 3))
    tt(DVE, WT.t[:, :, :], pb.t[:, :].rearrange("p (h t) -> p h t", h=4),
       bc(Umat.unsqueeze(1), [P, 4, 128]), ALU.mult, [pb, cst], [WT])
    pg.pop()

    rings = {}

    def mk_rings(tag, n):
        rings["sq"] = Ring([pg.sb(f"sq{tag}{i}", [P, n], BF16) for i in range(2)])
        rings["std"] = Ring([pg.sb(f"std{tag}{i}", [P, n], F32) for i in range(2)])

    def rms_rstd(src, sres, t0, n, nfeat_tiles=KC, denom=D):
        pb_ = psC.next()
        if n <= 128 and "sq8" in rings:
            sq = rings["sq8"].next()
            act(sq.t[:, :, 0:n], src[:, :, t0:t0 + n], AF.Square, sres, [sq])
            for k in range(nfeat_tiles):
                mm(pb_.t[:, 0:n], onesb, sq.t[:, k, 0:n], k == 0, k == nfeat_tiles - 1, [sq, cb16], [pb_],
                   sig=(k == nfeat_tiles - 1))
        else:
            for k in range(nfeat_tiles):
                sq = rings["sq"].next()
                act(sq.t[:, 0:n], src[:, k, t0:t0 + n], AF.Square, sres, [sq])
                mm(pb_.t[:, 0:n], onesb, sq.t[:, 0:n], k == 0, k == nfeat_tiles - 1, [sq, cb16], [pb_], sig=True)
        ckpt("n_mm")
        sd = rings["std"].next()
        act(sd.t[:, 0:n], pb_.t[:, 0:n], AF.Ln, [pb_], [sd], bias=EPS, scale=1.0 / denom)
        act(sd.t[:, 0:n], sd.t[:, 0:n], AF.Exp, [sd], [sd], scale=-0.5)
        return sd

    def norm_to(dst, dres, src, sres, t0, n, gcol0, d0=0):
        rs = rms_rstd(src, sres, t0, n)
        for k in range(KC):
            stt(dst[:, k, d0:d0 + n], src[:, k, t0:t0 + n], pcol.t[:, gcol0 + k:gcol0 + k + 1], rs.t[:, 0:n],
                ALU.mult, ALU.mult, sres + [pcol, rs], dres)

    def proj_fm(W, Wres, c0, hn, hnres, n, consume, kin=KC, ring=None):
        pb_ = (ring or psA).next()
        for k in range(kin):
            mm(pb_.t[:, 0:n], W[:, k, c0:c0 + 128], hn[:, k, 0:n], k == 0, k == kin - 1, Wres + hnres, [pb_],
               sig=(k == kin - 1))
        consume(pb_)

    pg.push()
    mk_rings("m", 128)
    rings["sq8"] = Ring([pg.sb(f"sq8{i}", [P, KC, 128], BF16) for i in range(1)])
    win = pg.sb("win", [P, KC, DIN], BF16)
    wout = pg.sb("wout", [P, KC, D], BF16)
    w_in_v = w_in.rearrange("(k p) n -> p k n", p=P)
    winR = [Res(f"win{c0}") for c0 in range(0, DIN, 256)]
    for i_ in (10, 0, 1, 4, 5, 2, 3, 6, 7, 8, 9):
        c0 = i_ * 256
        c1 = min(DIN, c0 + 256)
        ldc(win.t[:, :, c0:c1], w_in_v[:, :, c0:c1], [winR[i_]])
    w_out_v = w_out.rearrange("(k p) n -> p k n", p=P)
    woutR = []
    for c0 in range(0, D, 256):
        r = Res(f"wout{c0}")
        woutR.append(r)
        ldc(wout.t[:, :, c0:c0 + 256], w_out_v[:, :, c0:c0 + 256], [r])

    def winres(c0, c1):
        return [winR[i] for i in range(c0 // 256, (c1 - 1) // 256 + 1)]

    xin_ring = Ring([pg.sb(f"xin{i}", [P, D], F32) for i in range(1)])

    def load_tokens_fm(src_dram, row0, nrows, dst, dres, dcol0, preloaded=False):
        xin = xin_ring.bufs[0]
        if not preloaded:
            ld(xin.t[0:nrows, :], src_dram[row0:row0 + nrows, :], [xin])
        for half in range(2):
            pb_ = psC.next()
            for q in range(4):
                k = half * 4 + q
                tr(pb_.t[:, q * 128:q * 128 + nrows], xin.t[0:nrows, k * 128:(k + 1) * 128],
                   identf[0:nrows, 0:nrows], [xin, cst], [pb_], sig=(q == 3))
            cp(ACT if half == 0 else DVE, dst[:, half * 4:half * 4 + 4, dcol0:dcol0 + nrows],
               pb_.t[:, :].rearrange("p (q t) -> p q t", q=4)[:, :, 0:nrows], [pb_], dres)

    sm = Ring([pg.sb(f"sm{i}", [P, 8], F32) for i in range(4)])
    gv = pg.sb("gv", [P, 512], F32)
    vh = gv
    yv = pg.sb("yv", [P, 4, 128], F32)
    yz = yv
    ysq = pg.sb("ysq", [P, 4, 128], BF16)
    gstd = pg.sb("gstd", [P, 2, 128], F32)
    grs = pg.sb("grs", [P, 2, 128], F32)
    acc_ring = Ring([pg.sb(f"cacc{i}", [P, 4, TB], F32) for i in range(2)])
    for a_ in acc_ring.bufs:
        a_.rq = [Res(f"accq{q}") for q in range(4)]
    pg.push()
    slots = []
    for i_ in range(2):
        slots.append(dict(hn=pg.sb(f"hnb{i_}", [P, KC, TB], BF16), uT=pg.sb(f"uT{i_}", [P, 4, TB], F32),
                          zT=pg.sb(f"zT{i_}", [P, 4, TB], F32), xT=pg.sb(f"xT{i_}", [P, 4, TB], F32),
                          xTb=pg.sb(f"xTb{i_}", [P, 8, TB], BF16),
                          vb=pg.sb(f"vb{i_}", [P, 512], BF16), dt=pg.sb(f"dt{i_}", [P, 8], F32),
                          a=pg.sb(f"a{i_}", [P, 8], F32)))
    xpre = pg.sb("xpre", [P, 8, 3 + TB], F32)
    xpreR = [xpre.r, Res("xpre1")]
    mixT = pg.sb("mixT", [P, KC, TB], BF16)
    vf = pg.sb("vf", [P, 512], F32)
    hT = pg.sb("hT", [P, 512], F32)
    hTpad = pg.sb("hTpad", [P, 8, 128], BF16)
    xdtpad = pg.sb("xdtpad", [P, 8, 128], BF16)
    mset(POOL, hTpad.t[:, :, :], 0.0, [hTpad])
    mset(POOL, xdtpad.t[:, :, :], 0.0, [xdtpad])
    mset(POOL, xpre.t[:, :, 0:3], 0.0, xpreR)

    def padview(b):
        t_ = b.t
        return bass.AP(t_.tensor if hasattr(t_, "tensor") else t_, 0, [[8 * 128, P], [256, 4], [192, 2], [1, 64]])

    rhsA = pg.sb("rhsA", [P, 8, 128], F32)
    expo = pg.sb("expo", [P, 8, 128], F32)
    Lm = expo
    Em = rhsA
    MT = pg.sb("MT", [P, 8, 128], BF16)
    ChT = pg.sb("ChT", [P, 8, 128], BF16)
    xw = pg.sb("xw", [P, 512], BF16)
    Btok = pg.sb("Btok", [P, 256], BF16)
    acol = pg.sb("acol", [P, 8], F32)
    mxt = pg.sb("mxt", [P, 4, 128], F32)

    def softplus_dt(pb_, npart, dt_out, a_out):
        s1 = sm.next()
        tt(DVE, s1.t[0:npart, :], pb_.t[0:npart, 0:8], prow.t[0:npart, R_DTB:R_DTB + 8], ALU.add, [pb_, prow], [s1])
        act(s1.t[0:npart, :], s1.t[0:npart, :], AF.Exp, [s1], [s1])
        act(dt_out.t[0:npart, :], s1.t[0:npart, :], AF.Ln, [s1], [dt_out], bias=1.0)
        tt(DVE, a_out.t[0:npart, :], dt_out.t[0:npart, :], arow.t[0:npart, :], ALU.mult, [dt_out, arow], [a_out])

    mhalf = pg.sb("mhalf", [P, 1], F32)
    mset(POOL, mhalf.t[:, :], -0.5, [mhalf])

    def v_proj(hn, hnres, col0, npart):
        pb_ = psF.next()
        for k in range(KC):
            mm(pb_.t[0:npart, :], hn[:, k, col0:col0 + npart], win.t[:, k, 512:1024], k == 0, k == KC - 1,
               hnres + winres(512, 1024), [pb_], sig=(k == KC - 1))
        cp(ACT, gv.t[0:npart, :], pb_.t[0:npart, :], [pb_], [gv])

    def v_act(npart, vb_out, f32_out=None):
        s1 = sm.next()
        act(gv.t[0:npart, :], gv.t[0:npart, :], GELU, [gv], [gv, s1], accum=s1.t[0:npart, 0:1])
        ts(POOL, s1.t[0:npart, 1:2], s1.t[0:npart, 0:1], -1.0 / 512, None, ALU.mult, None, [s1], [s1])
        act(vb_out.t[0:npart, :], gv.t[0:npart, :], AF.Square, [gv, s1], [vb_out, s1], bias=s1.t[0:npart, 1:2],
            accum=s1.t[0:npart, 2:3])
        ts(POOL, s1.t[0:npart, 3:4], s1.t[0:npart, 2:3], 1.0 / 512, EPS, ALU.mult, ALU.add, [s1], [s1])
        tt(POOL, s1.t[0:npart, 4:5], s1.t[0:npart, 3:4], mhalf.t[0:npart, :], ALU.pow, [s1, mhalf], [s1])
        ts(DVE, vh.t[0:npart, :], gv.t[0:npart, :], s1.t[0:npart, 1:2], s1.t[0:npart, 4:5], ALU.add, ALU.mult,
           [gv, s1], [vh])
        tt(POOL, vh.t[0:npart, :], vh.t[0:npart, :], prow.t[0:npart, R_LNG:R_LNG + 512], ALU.mult, [vh, prow], [vh])
        tt(POOL, vb_out.t[0:npart, :], vh.t[0:npart, :], prow.t[0:npart, R_LNB:R_LNB + 512], ALU.add,
           [vh, prow], [vb_out])
        if f32_out is not None:
            tt(DVE, f32_out.t[0:npart, :], vh.t[0:npart, :], prow.t[0:npart, R_LNB:R_LNB + 512], ALU.add,
               [vh, prow], [f32_out])

    def v_tokmajor(hn, hnres, col0, npart, vb_out, f32_out=None):
        v_proj(hn, hnres, col0, npart)
        v_act(npart, vb_out, f32_out)

    def gate_norm(yv_ap, zap, n, R_extra, mix_dst, mres):
        tt(DVE, yz.t[:, :, 0:n], yv_ap, zap, ALU.mult, [yv] + R_extra, [yv])
        act(ysq.t[:, :, 0:n], yz.t[:, :, 0:n], AF.Square, [yz], [ysq])
        pb_ = psC.next()
        for g in range(2):
            for q in range(2):
                mm(pb_.t[:, g * 128:g * 128 + n], onesb, ysq.t[:, 2 * g + q, 0:n], q == 0, q == 1, [ysq, cb16], [pb_],
                   sig=(g == 1 and q == 1))
        act(gstd.t[:, :, 0:n], pb_.t[:, 0:256].rearrange("p (g t) -> p g t", g=2)[:, :, 0:n], AF.Ln, [pb_], [gstd],
            bias=EPS, scale=1.0 / 256)
        act(grs.t[:, :, 0:n], gstd.t[:, :, 0:n], AF.Exp, [gstd], [grs], scale=-0.5)
        for j in range(4):
            stt(mix_dst[:, 4 + j, 0:n], yz.t[:, j, 0:n], pcol.t[:, PC_SN + j:PC_SN + j + 1], grs.t[:, j // 2, 0:n],
                ALU.mult, ALU.mult, [yz, pcol, grs], mres)

    def conv_tile(pb_, f, n, xpre_ap_taps, xpre_res, wr_pre, out_f32, out_bf, ores):
        wr_pre(pb_)
        acc = acc_ring.next()
        act(acc.t[:, 0, 0:n], pb_.t[:, 0:n], AF.Identity, [pb_, pcol], [acc.rq[0]],
            bias=pcol.t[:, PC_CB + f:PC_CB + f + 1], scale=pcol.t[:, PC_CW + 24 + f:PC_CW + 24 + f + 1])
        for k in range(3):
            stt(acc.t[:, 0, 0:n], xpre_ap_taps(k), pcol.t[:, PC_CW + 8 * k + f:PC_CW + 8 * k + f + 1], acc.t[:, 0, 0:n],
                ALU.mult, ALU.add, [xpre_res, pcol, acc.rq[0]], [acc.rq[0]])
        if out_f32 is not None:
            act(out_f32, acc.t[:, 0, 0:n], AF.Silu, [acc.rq[0]], ores)
            cp(POOL, out_bf, out_f32, ores, ores)
        else:
            act(out_bf, acc.t[:, 0, 0:n], AF.Silu, [acc.rq[0]], ores)

    def wout_residual(mix, mixres, n, dstT, dres_fn, t0):
        for fo in range(KC):
            pb_ = psA.next()
            for k in range(KC):
                mm(pb_.t[:, 0:n], wout.t[:, k, fo * 128:(fo + 1) * 128], mix[:, k, 0:n], k == 0, k == KC - 1,
                   mixres + [woutR[fo // 2]], [pb_], sig=(k == KC - 1))
            tt(DVE, dstT[:, fo, t0:t0 + n], dstT[:, fo, t0:t0 + n], pb_.t[:, 0:n], ALU.add, [pb_] + dres_fn(),
               dres_fn())

    NBLK = SEQ // TB

    def front(b):
        S = slots[b % 2]
        t0 = b * TB
        hres = hpres(t0, TB)
        hn = S["hn"]
        load_tokens_fm(xp, t0, 128, hpT.t, hres, t0, preloaded=(b > 0))
        if b + 1 < NBLK:
            ld(xin_ring.bufs[0].t[:, :], xp[t0 + TB:t0 + 2 * TB, :], [xin_ring.bufs[0]])
        yield
        norm_to(hn.t, [hn], hpT.t, hres, t0, TB, PC_GMIX)
        yield
        def proj4(c0):
            pb_ = psF.next()
            for q in range(4):
                cc = c0 + q * 128
                for k in range(KC):
                    mm(pb_.t[:, q * 128:(q + 1) * 128], win.t[:, k, cc:cc + 128], hn.t[:, k, 0:TB], k == 0, k == KC - 1,
                       winres(cc, cc + 128) + [hn], [pb_], sig=(q == 3 and k == KC - 1))
            return pb_

        pbd = psB.next()
        for k in range(KC):
            mm(pbd.t[:, 0:8], hn.t[:, k, 0:128], win.t[:, k, 2560:2568], k == 0, k == KC - 1,
               [hn] + winres(2560, 2568), [pbd], sig=(k == KC - 1))
        softplus_dt(pbd, 128, S["dt"], S["a"])
        yield
        pb_ = proj4(0)
        cp(ACT, S["uT"].t[:, :, :].rearrange("p f t -> p (f t)"), pb_.t[:, :], [pb_], [S["uT"]])
        yield
        pb_ = proj4(1024)
        cp(ACT, S["zT"].t[:, :, :].rearrange("p f t -> p (f t)"), pb_.t[:, :], [pb_], [S["zT"]])
        yield
        v_proj(hn.t, [hn], 0, 128)
        yield
        S["acc"] = []
        for grp in range(2):
            f0 = grp * 4
            xr = [xpreR[grp]]
            pb_ = proj4(1536 + f0 * 128)
            if b > 0:
                cp(POOL, xpre.t[:, f0:f0 + 4, 0:3], xpre.t[:, f0:f0 + 4, TB:TB + 3], xr, xr)
            cp(ACT, xpre.t[:, f0:f0 + 4, 3:3 + TB], pb_.t[:, :].rearrange("p (f t) -> p f t", f=4), [pb_], xr)
            yield
            acc = acc_ring.next()
            S["acc"].append(acc)
            for q in range(4):
                f = f0 + q
                act(acc.t[:, q, :], xpre.t[:, f, 3:3 + TB], AF.Identity, xr + [pcol], [acc.rq[q]],
                    bias=pcol.t[:, PC_CB + f:PC_CB + f + 1], scale=pcol.t[:, PC_CW + 24 + f:PC_CW + 24 + f + 1])
                for k in range(3):
                    stt(acc.t[:, q, :], xpre.t[:, f, k:k + TB], pcol.t[:, PC_CW + 8 * k + f:PC_CW + 8 * k + f + 1],
                        acc.t[:, q, :], ALU.mult, ALU.add, xr + [pcol, acc.rq[q]], [acc.rq[q]])
                yield

    def burst(b):
        S = slots[b % 2]
        last = (b == NBLK - 1)
        u2 = S["uT"].t[:, :, :].rearrange("p f t -> p (f t)")
        act(u2, u2, GELU, [S["uT"]], [S["uT"]])
        v_act(128, S["vb"], vf if last else None)
        if last:
            st(gv_prompt[:, :], vf.t[:, :], [vf], vf.r)
        z2 = S["zT"].t[:, :, :].rearrange("p f t -> p (f t)")
        act(z2, z2, AF.Silu, [S["zT"]], [S["zT"]])
        a0, a1 = S["acc"]
        act(S["xT"].t[:, :, :], a0.t[:, :, :], AF.Silu, a0.rq, [S["xT"]] + a0.rq)
        act(S["xTb"].t[:, 4:8, :], a1.t[:, :, :], AF.Silu, a1.rq, [S["xTb"]] + a1.rq)
        act(S["xTb"].t[:, 0:4, :], a0.t[:, :, :], AF.Silu, a0.rq, [S["xTb"]] + a0.rq)

    def back(b):
        S = slots[b % 2]
        t0 = b * TB
        hres = hpres(t0, TB)
        uT, zT, xT, xTb, vb_, dt_, a_ = S["uT"], S["zT"], S["xT"], S["xTb"], S["vb"], S["dt"], S["a"]
        cs = slice(0, 128)
        fc = (b == 0)
        pxt = psB.next()
        pxb = pxt.t[:, :].bitcast(BF16)
        for f in range(6):
            tr(pxb[:, f * 128:(f + 1) * 128], xTb.t[:, f, cs], identb, [xTb, cb16], [pxt], sig=(f == 5))
        tt(DVE, rhsA.t[:, :, :], bc(Umat.unsqueeze(1), [P, 8, 128]), bc(a_.t[:, :].unsqueeze(2), [P, 8, 128]),
           ALU.mult, [cst, a_], [rhsA])
        yield
        pac0, pac1 = psK.next(), psK.next()
        mm(pac0.t[:, :], onesf, rhsA.t[:, 0:4, :].rearrange("p h t -> p (h t)"), True, True, [cst, rhsA], [pac0])
        mm(pac1.t[:, :], onesf, rhsA.t[:, 4:8, :].rearrange("p h t -> p (h t)"), True, True, [cst, rhsA], [pac1])
        pcol_ps = psC.next()
        mm(pcol_ps.t[:, 0:8], Umat, a_.t[:, :], True, True, [cst, a_], [pcol_ps])
        cp(DVE, acol.t[:, :], pcol_ps.t[:, 0:8], [pcol_ps], [acol])
        yield
        s2 = sm.next()
        for hh, pac in enumerate((pac0, pac1)):
            hs_ = slice(hh * 4, hh * 4 + 4)
            pv_ = pac.t[:, :].rearrange("p (h t) -> p h t", h=4)
            for q in range(4):
                h = hh * 4 + q
                stt(expo.t[:, h, :], pv_[:, q, :], acol.t[:, h:h + 1], negm, ALU.subtract, ALU.add,
                    [pac, acol, cst], [expo])
            tt(DVE, s2.t[:, hs_], pv_[:, :, 127], acol.t[:, hs_], ALU.subtract, [pac, acol], [s2])
            act(Em.t[:, hs_, :], pv_, AF.Exp, [pac], [Em])
            yield
        act(Lm.t[:, :, :], expo.t[:, :, :], AF.Exp, [expo], [expo])
        act(s2.t[:, :], s2.t[:, :], AF.Exp, [s2], [s2])
        tt(DVE, s2.t[:, :], s2.t[:, :], dt_.t[:, :], ALU.mult, [s2, dt_], [s2])
        yield
        pcb = psC.next()
        for g in range(2):
            mm(pcb.t[:, g * 128:(g + 1) * 128], xTb.t[:, 4 + g, cs], xTb.t[:, 6 + g, cs], True, True, [xTb], [pcb],
               sig=(g == 1))
        pcbv = pcb.t[:, 0:256].rearrange("p (g t) -> p g t", g=2)
        for g in range(2):
            tt(DVE, MT.t[:, 4 * g:4 * g + 4, :], Lm.t[:, 4 * g:4 * g + 4, :],
               bc(pcbv[:, g, :].unsqueeze(1), [P, 4, 128]), ALU.mult, [Lm, pcb], [MT])
        yield
        xtokv = pxb[:, 0:512].rearrange("p (h q) -> p h q", h=8)
        xdv = padview(xdtpad)
        tt(DVE, xdv, xtokv.rearrange("p (j e) q -> p j e q", e=2),
           bc(dt_.t[:, :].unsqueeze(2), [P, 8, 64]).rearrange("p (j e) q -> p j e q", e=2), ALU.mult,
           [pxt, dt_], [xdtpad])
        tt(DVE, xw.t[:, :].rearrange("p (h q) -> p h q", h=8), xtokv, bc(s2.t[:, :].unsqueeze(2), [P, 8, 64]),
           ALU.mult, [pxt, s2], [xw])
        cp(ACT, Btok.t[:, :], pxb[:, 512:768], [pxt], [Btok])
        yield
        pst = psB.next()
        for g in range(2):
            mm(pst.t[:, g * 256:(g + 1) * 256], Btok.t[:, g * 128:(g + 1) * 128], xw.t[:, g * 256:(g + 1) * 256],
               True, True, [Btok, xw], [pst], sig=(g == 1))
        if not fc:
            for g in range(2):
                tt(DVE, ChT.t[:, 4 * g:4 * g + 4, :], bc(xTb.t[:, 6 + g, cs].unsqueeze(1), [P, 4, 128]),
                   Em.t[:, 4 * g:4 * g + 4, :], ALU.mult, [xTb, Em], [ChT])
        yield
        py = psB.next()
        for j in range(4):
            for e in range(2):
                h = 2 * j + e
                mm(py.t[:, j * 128:(j + 1) * 128], xdtpad.t[:, h, :], MT.t[:, h, :], e == 0, fc and e == 1,
                   [xdtpad, MT], [py], sig=(fc and j == 3 and e == 1))
                if not fc:
                    mm(py.t[:, j * 128:(j + 1) * 128], hTpad.t[:, h, :], ChT.t[:, h, :], False, e == 1,
                       [hTpad, ChT], [py], sig=(j == 3 and e == 1))
        yield
        if fc:
            cp(DVE, hT.t[:, :], pst.t[:, :], [pst], [hT])
        else:
            tt(DVE, hT.t[:, :].rearrange("p (h q) -> p h q", h=8), hT.t[:, :].rearrange("p (h q) -> p h q", h=8),
               bc(Em.t[:, :, 127].unsqueeze(2), [P, 8, 64]), ALU.mult, [hT, Em], [hT])
            tt(DVE, hT.t[:, :], hT.t[:, :], pst.t[:, :], ALU.add, [hT, pst], [hT])
        cp(ACT, padview(hTpad), hT.t[:, :].rearrange("p (j e q) -> p j e q", j=4, e=2), [hT], [hTpad])
        yield
        for j in range(4):
            stt(yv.t[:, j, :], xT.t[:, j, cs], pcol.t[:, PC_D + j:PC_D + j + 1], py.t[:, j * 128:(j + 1) * 128],
                ALU.mult, ALU.add, [xT, pcol, py], [yv])
        yield
        gate_norm(yv.t[:, :, :], zT.t[:, :, cs], 128, [zT], mixT.t[:, :, cs], [mixT])
        yield
        pmx = psB.next()
        for h in range(4):
            mm(pmx.t[:, h * 128:(h + 1) * 128], vb_.t[:, h * 128:(h + 1) * 128], WT.t[:, h, :], True, True,
               [vb_, WT], [pmx], sig=(h == 3))
        tt(DVE, mxt.t[:, :, :], pmx.t[:, :].rearrange("p (h t) -> p h t", h=4),
           prow.t[:, R_BS:R_BS + 512].rearrange("p (h t) -> p h t", h=4), ALU.add, [pmx, prow], [mxt])
        tt(DVE, mixT.t[:, 0:4, cs], uT.t[:, :, cs], mxt.t[:, :, :], ALU.mult, [uT, mxt], [mixT])
        yield
        for half in range(2):
            pb_ = psK.next()
            for q in range(4):
                fo = half * 4 + q
                for k in range(KC):
                    mm(pb_.t[:, q * 128:(q + 1) * 128], wout.t[:, k, fo * 128:(fo + 1) * 128], mixT.t[:, k, 0:TB],
                       k == 0, k == KC - 1, [mixT, woutR[fo // 2]], [pb_], sig=(q == 3 and k == KC - 1))
            tt(DVE, hpT.t[:, half * 4:half * 4 + 4, t0:t0 + TB], hpT.t[:, half * 4:half * 4 + 4, t0:t0 + TB],
               pb_.t[:, :].rearrange("p (f t) -> p f t", f=4), ALU.add, [pb_] + hres, hres)
            yield

    def run_gens(gens):
        gens = list(gens)
        while gens:
            for g in list(gens):
                try:
                    next(g)
                except StopIteration:
                    gens.remove(g)

    if STOP == "setup":
        return finalize()
    def front_burst(b):
        for _ in front(b):
            yield
        burst(b)
        yield

    def run_weighted(ga, gb, na, nb):
        live = [ga, gb]
        while any(g is not None for g in live):
            for idx, n in ((0, na), (1, nb)):
                for _ in range(n):
                    if live[idx] is None:
                        break
                    try:
                        next(live[idx])
                    except StopIteration:
                        live[idx] = None

    run_gens([front_burst(0)])
    for b in range(NBLK):
        if b + 1 < NBLK:
            run_weighted(back(b), front_burst(b + 1), 1, 2)
        else:
            run_gens([back(b)])

    ssm_sb = pg.sb("ssm_sb", [P, 4, 128], F32)
    pb = psC.next()
    for j in range(4):
        tr(pb.t[:, j * 128:(j + 1) * 128], hT.t[:, j * 128:(j + 1) * 128], identf, [hT, cst], [pb], sig=(j == 3))
    cp(DVE, ssm_sb.t[:, :, :], pb.t[:, :].rearrange("p (j n) -> p j n", j=4), [pb], [ssm_sb])
    st(ssm_prompt.rearrange("(j p) n -> p j n", p=P), ssm_sb.t[:, :, :], [ssm_sb], ssm_sb.r)
    ctail = pg.sb("ctail", [P, D], F32)
    for half in range(2):
        pb = psC.next()
        for q in range(4):
            f = half * 4 + q
            tr(pb.t[0:3, q * 128:(q + 1) * 128], xpre.t[:, f, TB:TB + 3], identf, xpreR + [cst], [pb], sig=(q == 3))
        cp(DVE, ctail.t[0:3, half * 512:(half + 1) * 512], pb.t[0:3, :], [pb], [ctail])
    st(conv_prompt[:, :], ctail.t[0:3, :], [ctail], ctail.r)

    pg.pop()
    pg.push()
    if STOP == "mixer":
        return finalize()
    load_tokens_fm(xs, 0, NS, hsT.t, hsR, 0)
    hnS = pg.sb("hnS", [P, KC, NS], BF16)
    norm_to(hnS.t, [hnS], hsT.t, hsR, 0, NS, PC_GMIX)
    uS = pg.sb("uS", [P, 4, NS], F32)
    zS = pg.sb("zS", [P, 4, NS], F32)
    xS = pg.sb("xS", [P, 8, NS], F32)
    xSb = pg.sb("xSb", [P, 8, NS], BF16)
    xpS = pg.sb("xpS", [P, 8, NS, 4], F32)
    mixS = pg.sb("mixS", [P, KC, NS], BF16)
    scin = pg.sb("scin", [NS * 3, D], F32)
    ld(scin.t[:, :], sconv[:, :], [scin])
    for half in range(2):
        pb = psC.next()
        for q in range(4):
            f = half * 4 + q
            tr(pb.t[:, q * 128:q * 128 + 48], scin.t[0:48, f * 128:(f + 1) * 128], identf[0:48, 0:48], [scin, cst],
               [pb], sig=(q == 3))
        cp(DVE, xpS.t[:, half * 4:half * 4 + 4, :, 0:3],
           pb.t[:, :].rearrange("p (q t) -> p q t", q=4)[:, :, 0:48].rearrange("p q (b k) -> p q b k", k=3),
           [pb], [xpS])
    for f in range(8):
        c0 = 1536 + f * 128

        def consumeS(pb_, f=f):
            conv_tile(pb_, f, NS, lambda k: xpS.t[:, f, :, k], xpS,
                      lambda pb2: cp(DVE, xpS.t[:, f, :, 3], pb2.t[:, 0:NS], [pb2], [xpS]),
                      xS.t[:, f, :], xSb.t[:, f, :], [xS, xSb])
        proj_fm(win.t, winres(c0, c0 + 128), c0, hnS.t, [hnS], NS, consumeS)
    dtS = pg.sb("dtS", [P, 8], F32)
    aS = pg.sb("aS", [P, 8], F32)
    pb = psB.next()
    for k in range(KC):
        mm(pb.t[0:NS, 0:8], hnS.t[:, k, :], win.t[:, k, 2560:2568], k == 0, k == KC - 1, [hnS] + winres(2560, 2568),
           [pb], sig=(k == KC - 1))
    softplus_dt(pb, NS, dtS, aS)
    eS = pg.sb("eS", [NS, 2, 8, 64], F32)
    dcS = pg.sb("dcS", [P, 8], F32)
    act(dcS.t[0:NS, :], aS.t[0:NS, :], AF.Exp, [aS], [dcS])
    cp(DVE, eS.t[:, 0, :, :], bc(dtS.t[0:NS, :].unsqueeze(2), [NS, 8, 64]), [dtS], [eS])
    cp(DVE, eS.t[:, 1, :, :], bc(dcS.t[0:NS, :].unsqueeze(2), [NS, 8, 64]), [dcS], [eS])
    colS = pg.sb("colS", [P, 2, 4, NS], F32)
    eSf = eS.t[:, :, :, :].rearrange("b a h q -> b (a h q)")
    pb = psC.next()
    for q in range(8):
        tr(pb.t[:, q * NS:(q + 1) * NS], eSf[:, q * 128:(q + 1) * 128], identf[0:NS, 0:NS], [eS, cst], [pb],
           sig=(q == 7))
    cp(DVE, colS.t[:, :, :, :].rearrange("p a j b -> p (a j b)"), pb.t[:, 0:8 * NS], [pb], [colS])
    dtxS = pg.sb("dtxS", [P, 4, NS], F32)
    tt(DVE, dtxS.t[:, :, :], xS.t[:, 0:4, :], colS.t[:, 0, :, :], ALU.mult, [xS, colS], [dtxS])
    BCtok = pg.sb("BCtok", [NS, 512], F32)
    pb = psC.next()
    for q in range(4):
        tr(pb.t[0:NS, q * 128:(q + 1) * 128], xS.t[:, 4 + q, :], identf, [xS, cst], [pb], sig=(q == 3))
    cp(DVE, BCtok.t[:, :], pb.t[0:NS, :], [pb], [BCtok])
    id16 = cst.t[0:NS, C_ID:C_ID + NS]
    yS = pg.sb("yS", [P, 4, NS], F32)
    mset(DVE, yS.t[:, :, :], 0.0, [yS])
    hin_ring = Ring([pg.sb(f"hin{i}", [P, 4, 128], F32) for i in range(4)])
    hout_ring = Ring([pg.sb(f"hout{i}", [P, 4, 128], F32) for i in range(4)])
    t1_ring = Ring([pg.sb(f"t1{i}", [P, 128], F32) for i in range(4)])
    def sfiller():
        for f in range(4):
            proj_fm(win.t, winres(f * 128, f * 128 + 128), f * 128, hnS.t, [hnS], NS,
                    lambda pb_, f=f: act(uS.t[:, f, :], pb_.t[:, 0:NS], GELU, [pb_], [uS]))
        yield
        for f in range(4):
            proj_fm(win.t, winres(1024 + f * 128, 1024 + f * 128 + 128), 1024 + f * 128, hnS.t, [hnS], NS,
                    lambda pb_, f=f: act(zS.t[:, f, :], pb_.t[:, 0:NS], AF.Silu, [pb_], [zS]))
        yield
        scv = sconv.rearrange("(b k) c -> b k c", k=3)
        cov = conv_sample.rearrange("(b k) c -> b k c", k=3)
        pg.dma(SP, cov[:, 0:2, :], scv[:, 1:3, :], [], [], outsem)
        crow = pg.sb("crow", [NS, D], F32)
        for half in range(2):
            pb = psC.next()
            for q in range(4):
                f = half * 4 + q
                tr(pb.t[0:NS, q * 128:(q + 1) * 128], xpS.t[:, f, :, 3], identf, [xpS, cst], [pb], sig=(q == 3))
            cp(DVE, crow.t[:, half * 512:(half + 1) * 512], pb.t[0:NS, :], [pb], [crow])
        st(cov[:, 2, :], crow.t[:, :], [crow], crow.r)
        yield
        vbS = pg.sb("vbS", [P, 512], BF16)
        vfS = pg.sb("vfS", [P, 512], F32)
        v_tokmajor(hnS.t, [hnS], 0, NS, vbS, vfS)
        st(gv_sample[:, :], vfS.t[0:NS, :], [vfS], vfS.r)
        yield
        vTS = pg.sb("vTS", [P, 4, NS], F32)
        pb = psC.next()
        for h in range(4):
            tr(pb.t[:, h * NS:(h + 1) * NS], vfS.t[0:NS, h * 128:(h + 1) * 128], identf[0:NS, 0:NS], [vfS, cst], [pb],
               sig=(h == 3))
        for h in range(4):
            ts(DVE, vTS.t[:, h, :], pb.t[:, h * NS:(h + 1) * NS], pcol.t[:, PC_W00 + h:PC_W00 + h + 1],
               pcol.t[:, PC_B0 + h:PC_B0 + h + 1], ALU.mult, ALU.add, [pb, pcol], [vTS])
        tt(DVE, mixS.t[:, 0:4, :], uS.t[:, :, :], vTS.t[:, :, :], ALU.mult, [uS, vTS], [mixS])
        yield

    sfg = sfiller()
    hins = {}

    def hload(i):
        if i < NS:
            h_ = hin_ring.next()
            ld(h_.t[:, :, :], sssm[i].rearrange("(j p) n -> p j n", p=P), [h_])
            hins[i] = h_

    hload(0)
    hload(1)
    hload(2)
    for b_ in range(NS):
        hload(b_ + 3)
        hin = hins.pop(b_)
        pbc = psB.next()
        mm(pbc.t[:, :], bc(id16[:, b_:b_ + 1], [NS, 128]), BCtok.t[:, :], True, True, [cst, BCtok], [pbc])
        hout = hout_ring.next()
        for j in range(4):
            g = j // 2
            t1 = t1_ring.next()
            act(t1.t[:, :], hin.t[:, j, :], AF.Copy, [hin, colS], [t1], scale=colS.t[:, 1, j, b_:b_ + 1])
            stt(hout.t[:, j, :], pbc.t[:, g * 128:(g + 1) * 128], dtxS.t[:, j, b_:b_ + 1], t1.t[:, :], ALU.mult, ALU.add,
                [pbc, dtxS, t1], [hout])
            stt(t1.t[:, :], hout.t[:, j, :], 1.0, pbc.t[:, (2 + g) * 128:(3 + g) * 128], ALU.mult, ALU.mult,
                [hout, pbc], [t1, yS], accum=yS.t[:, j, b_:b_ + 1])
        st(ssm_sample[b_].rearrange("(j p) n -> p j n", p=P), hout.t[:, :, :], [hout], hout.r)
        next(sfg, None)
    for _ in sfg:
        pass
    yvS = pg.sb("yvS", [P, 4, NS], F32)
    for j in range(4):
        stt(yvS.t[:, j, :], xS.t[:, j, :], pcol.t[:, PC_D + j:PC_D + j + 1], yS.t[:, j, :], ALU.mult, ALU.add,
            [xS, pcol, yS], [yvS])
    tt(DVE, yv.t[:, :, 0:NS], yvS.t[:, :, :], yvS.t[:, :, :], ALU.max, [yvS], [yv])
    gate_norm(yv.t[:, :, 0:NS], zS.t[:, :, :], NS, [zS], mixS.t, [mixS])
    wout_residual(mixS.t, [mixS], NS, hsT.t, lambda: hsR, 0)
    pg.pop()
    pg.pop()
    del rings["sq8"]

    if STOP == "mixer_all":
        return finalize()
    pg.push()
    wq = pg.sb("wq", [P, KC, D], BF16)
    wxo = pg.sb("wxo", [P, KC, D], BF16)
    KT = pg.sb("KT", [P, KC, NMEM], BF16)
    Vb = pg.sb("Vb", [P, 2, D], BF16)
    smx = pg.sb("smx", [P, 16], F32)
    mk_rings("x", TBX)

    def load_w(dst, src, nslab=4):
        ldc(dst.t[:, :, :], src.rearrange("(k p) n -> p k n", p=P), [dst])

    pg.push()
    wk = pg.sb("wk", [P, KC, D], BF16)
    wv = pg.sb("wv", [P, KC, D], BF16)
    load_w(wk, w_k)
    load_w(wv, w_v)
    load_w(wq, w_q)
    load_w(wxo, w_xo)
    mem = pg.sb("mem", [P, 2, D], F32)
    mn = pg.sb("mn", [P, 2, D], BF16)
    mnT = pg.sb("mnT", [P, KC, NMEM], BF16)
    ld(mem.t[:, :, :], memp.rearrange("(mt p) c -> p mt c", p=P), [mem])
    for mt in range(2):
        act(mn.t[:, mt, :], mem.t[:, mt, :], AF.Square, [mem], [mn, smx], accum=smx.t[:, mt:mt + 1])
    act(smx.t[:, 2:4], smx.t[:, 0:2], AF.Sqrt, [smx], [smx], bias=EPS, scale=1.0 / D)
    recip(smx.t[:, 4:6], smx.t[:, 2:4], [smx], [smx])
    for mt in range(2):
        ts(DVE, mn.t[:, mt, :], mem.t[:, mt, :], smx.t[:, 4 + mt:5 + mt], None, ALU.mult, None, [mem, smx], [mn])
    for mt in range(2):
        for half in range(2):
            pb = psC.next()
            pbb = pb.t[:, :].bitcast(BF16)
            for q in range(4):
                k = half * 4 + q
                tr(pbb[:, q * 128:(q + 1) * 128], mn.t[:, mt, k * 128:(k + 1) * 128], identb, [mn, cb16], [pb],
                   sig=(q == 3))
            tt(DVE, mnT.t[:, half * 4:half * 4 + 4, mt * 128:(mt + 1) * 128],
               pbb[:, 0:512].rearrange("p (q t) -> p q t", q=4),
               bc(pcol.t[:, PC_GM + half * 4:PC_GM + half * 4 + 4].unsqueeze(2), [P, 4, 128]), ALU.mult,
               [pb, pcol], [mnT])
    for fo in range(KC):
        proj_fm(wk.t, [wk], fo * 128, mnT.t, [mnT], NMEM,
                lambda pb_, fo=fo: cp(ACT, KT.t[:, fo, :], pb_.t[:, 0:NMEM], [pb_], [KT]))
    kv_ring = Ring([pg.sb(f"kvst{i}", [P, D], F32) for i in range(2)])
    for (W_, outd, isV) in ((wk, mk_o, False), (wv, mv_o, True)):
        for mt in range(2):
            kv = kv_ring.next()
            for half in range(2):
                pb = psA.next()
                for k in range(KC):
                    mm(pb.t[:, :], mnT.t[:, k, mt * 128:(mt + 1) * 128], W_.t[:, k, half * 512:(half + 1) * 512],
                       k == 0, k == KC - 1, [mnT, W_], [pb], sig=(k == KC - 1))
                cp(ACT if half == 0 else DVE, kv.t[:, half * 512:(half + 1) * 512], pb.t[:, :], [pb], [kv])
            if isV:
                cp(ACT, Vb.t[:, mt, :], kv.t[:, :], [kv], [Vb])
            st(outd[mt * 128:(mt + 1) * 128, :], kv.t[:, :], [kv], kv.r)
    pg.pop()
    xslots = [dict(hnx=pg.sb(f"hnx{i}", [P, KC, TBX], BF16), qT=pg.sb(f"qT{i}", [P, KC, TBX], BF16)) for i in range(2)]
    oT = pg.sb("oT", [P, KC, TBX], BF16)
    es_ring = Ring([pg.sb(f"es{i}", [P, 2, TBX], BF16) for i in range(2)])
    rs_ring = Ring([pg.sb(f"rsx{i}", [P, TBX], F32) for i in range(2)])

    def xo_residual_g(o_, ores, n, dstT, dres, t0):
        for fo in range(KC):
            pb_ = psA.next()
            for k in range(KC):
                mm(pb_.t[:, 0:n], wxo.t[:, k, fo * 128:(fo + 1) * 128], o_[:, k, 0:n], k == 0, k == KC - 1,
                   ores + [wxo], [pb_], sig=(k == KC - 1))
            tt(DVE, dstT[:, fo, t0:t0 + n], dstT[:, fo, t0:t0 + n], pb_.t[:, 0:n], ALU.add, [pb_] + dres, dres)
            yield

    def xfront(b):
        S = xslots[b % 2]
        t0 = b * TBX
        hres = hpres(t0, TBX)
        hnx, qT = S["hnx"], S["qT"]
        norm_to(hnx.t, [hnx], hpT.t, hres, t0, TBX, PC_GX)
        yield
        for fo in range(KC):
            proj_fm(wq.t, [wq], fo * 128, hnx.t, [hnx], TBX,
                    lambda pb_, fo=fo: act(qT.t[:, fo, :], pb_.t[:, :], AF.Copy, [pb_], [qT], scale=0.0625))
            yield

    def xback(b):
        S = xslots[b % 2]
        t0 = b * TBX
        hres = hpres(t0, TBX)
        qT = S["qT"]
        for h in range(4):
            e_ = es_ring.next()
            for mt in range(2):
                pb = psA.next()
                for dc in range(2):
                    mm(pb.t[:, :], KT.t[:, 2 * h + dc, mt * 128:(mt + 1) * 128], qT.t[:, 2 * h + dc, :], dc == 0, dc == 1,
                       [KT, qT], [pb], sig=(dc == 1))
                act(e_.t[:, mt, :], pb.t[:, :], AF.Exp, [pb], [e_])
            yield
            pbs = psC.next()
            for mt in range(2):
                mm(pbs.t[:, :], onesb, e_.t[:, mt, :], mt == 0, mt == 1, [e_, cb16], [pbs], sig=(mt == 1))
            rs = rs_ring.next()
            act(rs.t[:, :], pbs.t[:, :], AF.Ln, [pbs], [rs])
            act(rs.t[:, :], rs.t[:, :], AF.Exp, [rs], [rs], scale=-1.0)
            yield
            for dc in range(2):
                pb = psA.next()
                for mt in range(2):
                    mm(pb.t[:, :], Vb.t[:, mt, (2 * h + dc) * 128:(2 * h + dc + 1) * 128], e_.t[:, mt, :], mt == 0, mt == 1,
                       [Vb, e_], [pb], sig=(mt == 1))
                tt(DVE, oT.t[:, 2 * h + dc, :], pb.t[:, :], rs.t[:, :], ALU.mult, [pb, rs], [oT])
            yield
        for _ in xo_residual_g(oT.t, [oT], TBX, hpT.t, hres, t0):
            yield

    hnSx = pg.sb("hnSx", [P, KC, NS], BF16)
    qtok = pg.sb("qtok", [NS, D], BF16)
    sS = pg.sb("sS", [P, 2, 4 * NS], F32)
    jx = pg.sb("jx", [P, 256], F32)
    K_ring = Ring([pg.sb(f"Kc{i}", [P, D], F32) for i in range(3)])
    pe_ = pg.sb("pe_", [64, 256], F32)
    pTb = pg.sb("pTb", [P, 2, 64], BF16)
    V_ring = Ring([pg.sb(f"Vc{i}", [P, D], BF16) for i in range(3)])
    oST = pg.sb("oST", [P, KC, NS], BF16)

    def sample_attn():
        norm_to(hnSx.t, [hnSx], hsT.t, hsR, 0, NS, PC_GX)
        yield
        for half in range(2):
            pb = psA.next()
            for k in range(KC):
                mm(pb.t[0:NS, :], hnSx.t[:, k, :], wq.t[:, k, half * 512:(half + 1) * 512], k == 0, k == KC - 1,
                   [hnSx, wq], [pb], sig=(k == KC - 1))
            act(qtok.t[:, half * 512:(half + 1) * 512], pb.t[0:NS, :], AF.Copy, [pb], [qtok], scale=0.0625)
            yield
        mset(DVE, sS.t[:, :, :], 0.0, [sS])
        kbufs = {}

        def kload(i):
            if i < 2 * NS:
                kb_ = K_ring.next()
                ld(kb_.t[:, :], ck[i // 2][(i % 2) * 128:(i % 2 + 1) * 128, :], [kb_])
                kbufs[i] = kb_

        kload(0)
        kload(1)
        for b_ in range(NS):
            for mt in range(2):
                i = 2 * b_ + mt
                kload(i + 2)
                Kb = kbufs.pop(i)
                if mt == 0:
                    yield
                if mt == 0:
                    pq = [psA.next(), psA.next()]
                    for half in range(2):
                        mm(pq[half].t[:, :], bc(identb[0:NS, b_:b_ + 1], [NS, 128]), qtok.t[:, half * 512:(half + 1) * 512],
                           True, True, [cb16, qtok], [pq[half]])
                for h in range(4):
                    stt(jx.t[:, :], Kb.t[:, h * 256:(h + 1) * 256], 1.0, pq[h // 2].t[:, (h % 2) * 256:(h % 2 + 1) * 256],
                        ALU.mult, ALU.mult, [Kb, pq[h // 2]], [jx, sS], accum=sS.t[:, mt, b_ * 4 + h:b_ * 4 + h + 1])
            yield
        pt = psC.next()
        for mt in range(2):
            tr(pt.t[0:64, mt * 128:(mt + 1) * 128], sS.t[:, mt, :], identf, [sS, cst], [pt], sig=(mt == 1))
        pg.op(DVE, lambda: nc.vector.reduce_max(out=smx.t[0:64, 8:9], in_=pt.t[0:64, 0:256], axis=AX.X), [pt.r], [smx.r])
        ts(DVE, smx.t[0:64, 9:10], smx.t[0:64, 8:9], -1.0, None, ALU.mult, None, [smx], [smx])
        act(pe_.t[:, :], pt.t[0:64, 0:256], AF.Exp, [pt, smx], [pe_, smx], bias=smx.t[0:64, 9:10], accum=smx.t[0:64, 10:11])
        recip(smx.t[0:64, 11:12], smx.t[0:64, 10:11], [smx], [smx])
        ts(DVE, pe_.t[:, :], pe_.t[:, :], smx.t[0:64, 11:12], None, ALU.mult, None, [pe_, smx], [pe_])
        yield
        pt2 = psC.next()
        for mt in range(2):
            tr(pt2.t[:, mt * 64:(mt + 1) * 64], pe_.t[0:64, mt * 128:(mt + 1) * 128], identf[0:64, 0:64], [pe_, cst], [pt2],
               sig=(mt == 1))
        cp(DVE, pTb.t[:, :, :], pt2.t[:, 0:128].rearrange("p (m c) -> p m c", m=2), [pt2], [pTb])
        yield
        poS = psB.next()
        vbufs = {}

        def vload(i):
            if i < 2 * NS:
                vb_ = V_ring.next()
                ldc(vb_.t[:, :], cv[i // 2][(i % 2) * 128:(i % 2 + 1) * 128, :], [vb_])
                vbufs[i] = vb_

        vload(0)
        vload(1)
        for b_ in range(NS):
            for mt in range(2):
                i = 2 * b_ + mt
                vload(i + 2)
                Vc = vbufs.pop(i)
                yield
                for c in range(KC):
                    h = c // 2
                    col = mt * KC * NS + c * NS + b_
                    mm(poS.t[:, col:col + 1], Vc.t[:, c * 128:(c + 1) * 128], pTb.t[:, mt, b_ * 4 + h:b_ * 4 + h + 1],
                       True, True, [Vc, pTb], [poS], sig=(c == KC - 1))
        yield
        cp(DVE, jx.t[:, 0:KC * NS], poS.t[:, 0:KC * NS], [poS], [jx])
        tt(DVE, oST.t[:, :, :], jx.t[:, 0:KC * NS].rearrange("p (c b) -> p c b", c=KC),
           poS.t[:, KC * NS:2 * KC * NS].rearrange("p (c b) -> p c b", c=KC), ALU.add, [poS, jx], [oST])
        for _ in xo_residual_g(oST.t, [oST], NS, hsT.t, hsR, 0):
            yield

    def run_with_filler(main, filler):
        main = list(main)
        while main:
            for g in list(main):
                try:
                    next(g)
                except StopIteration:
                    main.remove(g)
            if filler[0] is not None:
                try:
                    next(filler[0])
                except StopIteration:
                    filler[0] = None

    NXB = SEQ // TBX
    sfill = [sample_attn()]
    run_with_filler([xfront(0)], sfill)
    for b in range(NXB):
        gs = [xback(b)]
        if b + 1 < NXB:
            gs.append(xfront(b + 1))
        run_with_filler(gs, sfill)
    if sfill[0] is not None:
        run_gens([sfill[0]])
    pg.pop()
    if STOP == "xattn_s":
        return finalize()

    pg.push()
    mk_rings("f", TBX)
    hnF = pg.sb("hnF", [P, KC, SEQ], BF16)
    hnFR = [Res(f"hnF{i}") for i in range(SEQ // TBX)]
    hnSF = pg.sb("hnSF", [P, KC, NS], BF16)
    for b in range(SEQ // TBX):
        norm_to(hnF.t, [hnFR[b]], hpT.t, hpres(b * TBX, TBX), b * TBX, TBX, PC_GF, d0=b * TBX)
    norm_to(hnSF.t, [hnSF], hsT.t, hsR, 0, NS, PC_GF)
    NH = NJ // 2
    actT = pg.sb("actT", [P, NH, SEQ], BF16)
    actR = [Res(f"act{i}") for i in range(SEQ // TBX)]
    actS = pg.sb("actS", [P, NH, NS], BF16)
    wd = pg.sb("wd", [P, NH, D], BF16)
    wg_ring = Ring([pg.sb(f"wg{i}", [P, KC, 128], BF16) for i in range(2)])
    wu_ring = Ring([pg.sb(f"wu{i}", [P, KC, 128], BF16) for i in range(2)])
    sg_ring = Ring([pg.sb(f"sg{i}", [P, TBX], F32) for i in range(2)])
    w_gate_v = w_gate.rearrange("(k p) n -> p k n", p=P)
    w_up_v = w_up.rearrange("(k p) n -> p k n", p=P)
    nfrow = pg.sb("nfrow", [P, D], F32)
    ld(nfrow.t[:, :], nf_d[:, :], [nfrow])
    yt_ring = Ring([pg.sb(f"yt{i}", [P, D], F32) for i in range(2)])
    sq2 = pg.sb("sq2", [P, 512], F32)
    fs_ring = Ring([pg.sb(f"fs{i}", [P, 8], F32) for i in range(2)])

    def final_norm(srcT, sres, c0, n, out_ap):
        yt = yt_ring.next()
        fs = fs_ring.next()
        for half in range(2):
            pb_ = psC.next()
            for q in range(4):
                tr(pb_.t[0:n, q * 128:(q + 1) * 128], srcT[:, half * 4 + q, c0:c0 + n], identf, sres + [cst], [pb_],
                   sig=(q == 3))
            cp(DVE, yt.t[0:n, half * 512:(half + 1) * 512], pb_.t[0:n, :], [pb_], [yt])
            act(sq2.t[0:n, :], pb_.t[0:n, :], AF.Square, [pb_], [sq2, fs], accum=fs.t[0:n, half:half + 1])
        tt(DVE, fs.t[0:n, 2:3], fs.t[0:n, 0:1], fs.t[0:n, 1:2], ALU.add, [fs], [fs])
        act(fs.t[0:n, 3:4], fs.t[0:n, 2:3], AF.Sqrt, [fs], [fs], bias=EPS, scale=1.0 / D)
        recip(fs.t[0:n, 4:5], fs.t[0:n, 3:4], [fs], [fs])
        stt(yt.t[0:n, :], yt.t[0:n, :], fs.t[0:n, 4:5], nfrow.t[0:n, :], ALU.mult, ALU.mult, [yt, fs, nfrow], [yt])
        st(out_ap, yt.t[0:n, :], [yt], yt.r)

    for hf in range(2):
        j0 = hf * NH
        for jj in range(NH):
            j = j0 + jj
            g_ = wg_ring.next()
            u_ = wu_ring.next()
            ldc(g_.t[:, :, :], w_gate_v[:, :, j * 128:(j + 1) * 128], [g_])
            ldc(u_.t[:, :, :], w_up_v[:, :, j * 128:(j + 1) * 128], [u_])
            if jj == 0:
                wdv = w_down[j0 * 128:(j0 + NH) * 128, :].rearrange("(j p) n -> p j n", p=P)
                ldc(wd.t[:, :, :], wdv[:, :, :], [wd])
            for tb in range(SEQ // TBX):
                tsl = slice(tb * TBX, (tb + 1) * TBX)
                pg_ = psA.next()
                for k in range(KC):
                    mm(pg_.t[:, :], g_.t[:, k, :], hnF.t[:, k, tsl], k == 0, k == KC - 1, [g_, hnFR[tb]], [pg_],
                       sig=(k == KC - 1))
                pu_ = psA.next()
                for k in range(KC):
                    mm(pu_.t[:, :], u_.t[:, k, :], hnF.t[:, k, tsl], k == 0, k == KC - 1, [u_, hnFR[tb]], [pu_],
                       sig=(k == KC - 1))
                sg = sg_ring.next()
                act(sg.t[:, :], pg_.t[:, :], AF.Silu, [pg_], [sg])
                tt(DVE, actT.t[:, jj, tsl], sg.t[:, :], pu_.t[:, :], ALU.mult, [sg, pu_], [actR[tb]])
            pS = psB.next()
            for k in range(KC):
                mm(pS.t[:, 0:NS], g_.t[:, k, :], hnSF.t[:, k, :], k == 0, k == KC - 1, [g_, hnSF], [pS], sig=False)
            for k in range(KC):
                mm(pS.t[:, NS:2 * NS], u_.t[:, k, :], hnSF.t[:, k, :], k == 0, k == KC - 1, [u_, hnSF], [pS],
                   sig=(k == KC - 1))
            sg = sg_ring.next()
            act(sg.t[:, 0:NS], pS.t[:, 0:NS], AF.Silu, [pS], [sg])
            tt(DVE, actS.t[:, jj, :], sg.t[:, 0:NS], pS.t[:, NS:2 * NS], ALU.mult, [sg, pS], [actS])
        for tb in range(SEQ // TBX):
            tsl = slice(tb * TBX, (tb + 1) * TBX)
            hres = hpres(tb * TBX, TBX)
            for fo in range(KC):
                pb = psA.next()
                for jj in range(NH):
                    mm(pb.t[:, :], wd.t[:, jj, fo * 128:(fo + 1) * 128], actT.t[:, jj, tsl], jj == 0, jj == NH - 1,
                       [wd, actR[tb]], [pb], sig=(jj == NH - 1))
                tt(DVE, hpT.t[:, fo, tsl], hpT.t[:, fo, tsl], pb.t[:, :], ALU.add, [pb] + hres, hres)
            if hf == 1:
                for i in range(tb * (TBX // 128), (tb + 1) * (TBX // 128)):
                    final_norm(hpT.t, hpres(i * 128, 128), i * 128, 128, y_prompt[i * 128:(i + 1) * 128, :])
        for fo in range(KC):
            pb = psB.next()
            for jj in range(NH):
                mm(pb.t[:, 0:NS], wd.t[:, jj, fo * 128:(fo + 1) * 128], actS.t[:, jj, :], jj == 0, jj == NH - 1,
                   [wd, actS], [pb], sig=(jj == NH - 1))
            tt(DVE, hsT.t[:, fo, :], hsT.t[:, fo, :], pb.t[:, 0:NS], ALU.add, [pb] + hsR, hsR)
        if hf == 1:
            final_norm(hsT.t, hsR, 0, NS, y_sample[:, :])
    pg.pop()
    if STOP == "ffn":
        return finalize()

    return finalize()


_NC_CACHE = {}


def kernel(**inp):
    inp = {k: np.asarray(v) for k, v in inp.items()}
    if "nc" not in _NC_CACHE:
        _NC_CACHE["nc"] = build_program()
    nc = _NC_CACHE["nc"]
    prow, pcol, nf = _tables(inp)
    consts = _consts()
    shared = {
        "w_in": np.ascontiguousarray(inp["w_in"][0]), "w_out": np.ascontiguousarray(inp["w_out"][0]),
        "w_q": np.ascontiguousarray(inp["w_q"][0]), "w_k": np.ascontiguousarray(inp["w_k"][0]),
        "w_v": np.ascontiguousarray(inp["w_v"][0]), "w_xo": np.ascontiguousarray(inp["w_xo"][0]),
        "w_gate": np.ascontiguousarray(inp["w_gate"][0]), "w_up": np.ascontiguousarray(inp["w_up"][0]),
        "w_down": np.ascontiguousarray(inp["w_down"][0]),
        "gws": np.ascontiguousarray(inp["gmlp_ws"][0].reshape(512, 128)),
        "consts": consts, "prow": prow, "pcol": pcol, "nfrow": nf,
    }
    in_maps = []
    for c in range(NCORES):
        s = slice(c * NS, (c + 1) * NS)
        m = dict(shared)
        m["xp"] = np.ascontiguousarray(inp["x_prompt"][c])
        m["memp"] = np.ascontiguousarray(inp["mem_prompt"][c])
        m["xs"] = np.ascontiguousarray(inp["x_sample"][s, 0, :])
        m["sssm"] = np.ascontiguousarray(inp["state_ssm"][0, s].reshape(NS, 512, 128))
        m["sconv"] = np.ascontiguousarray(inp["state_conv"][0, s].reshape(NS * 3, D))
        m["ck"] = np.ascontiguousarray(inp["cache_mem_k"][0, s].reshape(NS, NMEM, D))
        m["cv"] = np.ascontiguousarray(inp["cache_mem_v"][0, s].reshape(NS, NMEM, D))
        in_maps.append(m)
    res = run_bass_kernel_spmd(nc, in_maps, core_ids=list(range(NCORES)))
    R = res.results
    _NC_CACHE["last"] = R

    def cat(name, shape_per):
        return np.stack([np.asarray(R[c][name], np.float32).reshape(shape_per) for c in range(NCORES)], 0)

    y_prompt = cat("y_prompt", (SEQ, D))
    y_sample = cat("y_sample", (NS, D)).reshape(NCORES * NS, 1, D)
    ssm_p = cat("ssm_prompt", (8, 64, 128))[None]
    conv_p = cat("conv_prompt", (3, D))[None]
    gv_p = cat("gv_prompt", (128, 512))[None]
    mk = cat("mk_o", (NMEM, 4, 256))[None]
    mv = cat("mv_o", (NMEM, 4, 256))[None]
    ssm_s = cat("ssm_sample", (NS, 8, 64, 128)).reshape(1, NCORES * NS, 8, 64, 128)
    conv_s = cat("conv_sample", (NS, 3, D)).reshape(1, NCORES * NS, 3, D)
    gv_s = cat("gv_sample", (NS, 512)).reshape(1, NCORES * NS, 1, 512)
    return (y_prompt, y_sample, ssm_p, conv_p, gv_p, mk, mv, ssm_s, conv_s, gv_s)
```
